# Optimizing a Trainium2 kernel written in Bass

```python
import jax, jax.numpy as jnp
from jax import lax
import numpy as np

D_MODEL = 2048
BATCH = 1
SEQ = 8192
DEPTH = 2

D_MIX = D_MODEL
GROUP = D_MIX // 4
MLA_HEADS = 4
MLA_NOPE = 128
MLA_ROPE = 64
MLA_V = GROUP // MLA_HEADS
MLA_QK = MLA_NOPE + MLA_ROPE
Q_LORA = 512
KV_LORA = 256
ROPE_BASE = 10000.0
ATT_BLOCK = 128
SC_WIDTH = GROUP
SC_K = 3
CF_WIDTH = GROUP
CF_K = 31
RET_HEADS = 4
RET_DV = GROUP // RET_HEADS
RET_DK = RET_DV // 2
RET_CHUNK = 128
D_FF = 5632
FFN_K = 3
PLE_DIM = 256
EPS = 1e-6

IN_SIZES = (Q_LORA, KV_LORA, MLA_ROPE,
            SC_WIDTH, SC_WIDTH, SC_WIDTH,
            CF_WIDTH, CF_WIDTH,
            RET_HEADS * RET_DK, RET_HEADS * RET_DK,
            RET_HEADS * RET_DV, RET_HEADS * RET_DV)
N_IN = sum(IN_SIZES)

kernel_name = "hybrid_parallel_mixer_block"

F32 = jnp.float32


def split_cols(z, sizes):
    idx = np.cumsum(np.array(sizes))[:-1].tolist()
    return jnp.split(z, idx, axis=-1)


def rmsnorm(x, g):
    xf = x.astype(F32)
    y = xf * lax.rsqrt(jnp.mean(xf * xf, axis=-1, keepdims=True) + EPS)
    return (y * g.astype(F32)).astype(x.dtype)


def layernorm(x, g, b):
    xf = x.astype(F32)
    mu = jnp.mean(xf, axis=-1, keepdims=True)
    xc = xf - mu
    y = xc * lax.rsqrt(jnp.mean(xc * xc, axis=-1, keepdims=True) + EPS)
    return (y * g.astype(F32) + b.astype(F32)).astype(x.dtype)


def causal_dwconv(x, w, b=None):
    K, C = w.shape
    y = lax.conv_general_dilated(x, w[:, None, :].astype(x.dtype), window_strides=(1,),
                                 padding=[(K - 1, 0)], dimension_numbers=('NWC', 'WIO', 'NWC'),
                                 feature_group_count=C)
    if b is not None:
        y = y + b.astype(y.dtype)
    return y


def rotate(x, pos, inv_freq):
    ang = pos.astype(F32)[:, :, None] * inv_freq[None, None, :]
    cos = jnp.cos(ang)[:, :, None, :]
    sin = jnp.sin(ang)[:, :, None, :]
    x1, x2 = jnp.split(x.astype(F32), 2, axis=-1)
    return jnp.concatenate([x1 * cos - x2 * sin, x2 * cos + x1 * sin], axis=-1).astype(x.dtype)


def causal_attention_blocked(q, k, v):
    B, S, H, Dq = q.shape
    Dv = v.shape[-1]
    nb = S // ATT_BLOCK
    scale = Dq ** -0.5
    qb = q.reshape(B, nb, ATT_BLOCK, H, Dq).transpose(1, 0, 2, 3, 4)
    k_idx = jnp.arange(S)

    def one_block(args):
        blk, qblk = args
        q_idx = blk * ATT_BLOCK + jnp.arange(ATT_BLOCK)
        s = jnp.einsum('bqhd,bkhd->bhqk', qblk, k, preferred_element_type=F32) * scale
        s = jnp.where(k_idx[None, :] <= q_idx[:, None], s, jnp.finfo(F32).min)
        pr = jax.nn.softmax(s, axis=-1)
        return jnp.einsum('bhqk,bkhd->bqhd', pr.astype(v.dtype), v)

    out = lax.map(one_block, (jnp.arange(nb), qb))
    return out.transpose(1, 0, 2, 3, 4).reshape(B, S, H, Dv)


def retention_chunkwise(q, k, v):
    B, S, H, dk = q.shape
    dv = v.shape[-1]
    C = RET_CHUNK
    n = S // C
    to_chunks = lambda t: t.astype(F32).reshape(B, n, C, H, t.shape[-1]).transpose(1, 0, 3, 2, 4)
    qc, kc, vc = to_chunks(q), to_chunks(k), to_chunks(v)
    lg = jnp.log(1.0 - 2.0 ** (-5.0 - jnp.arange(H, dtype=F32)))[:, None]
    idx = jnp.arange(C, dtype=F32)
    rel = idx[:, None] - idx[None, :]
    inner_decay = jnp.where(rel[None] >= 0, jnp.exp(lg[:, :, None] * rel[None]), 0.0)
    q_decay = jnp.exp(lg * (idx + 1.0))[None, :, :, None]
    k_decay = jnp.exp(lg * (C - 1.0 - idx))[None, :, :, None]
    chunk_decay = jnp.exp(lg[:, 0] * C)[None, :, None, None]

    def step(state, inp):
        qb, kb, vb = inp
        sc = jnp.einsum('bhqd,bhkd->bhqk', qb, kb) * inner_decay[None]
        inner = jnp.einsum('bhqk,bhkv->bhqv', sc, vb)
        cross = jnp.einsum('bhqd,bhdv->bhqv', qb, state) * q_decay
        new_state = state * chunk_decay + jnp.einsum('bhkd,bhkv->bhdv', kb * k_decay, vb)
        return new_state, inner + cross

    state0 = jnp.zeros((B, H, dk, dv), F32)
    _, out = lax.scan(step, state0, (qc, kc, vc))
    return out.transpose(1, 0, 3, 2, 4).reshape(B, S, H, dv)


def mla_group(zq, zkv, zkr, pos, g_qa, g_kva, w_q_up, w_kv_up, g_qn, g_kn):
    B, S, _ = zq.shape
    c_q = rmsnorm(zq, g_qa)
    c_kv = rmsnorm(zkv, g_kva)
    q = (c_q @ w_q_up).reshape(B, S, MLA_HEADS, MLA_QK)
    kv = (c_kv @ w_kv_up).reshape(B, S, MLA_HEADS, MLA_NOPE + MLA_V)
    k_nope, v = kv[..., :MLA_NOPE], kv[..., MLA_NOPE:]
    k_rope = jnp.broadcast_to(zkr[:, :, None, :], (B, S, MLA_HEADS, MLA_ROPE))
    k = jnp.concatenate([k_nope, k_rope], axis=-1)
    q = rmsnorm(q, g_qn)
    k = rmsnorm(k, g_kn)
    inv = ROPE_BASE ** (-jnp.arange(0, MLA_ROPE, 2, dtype=F32) / MLA_ROPE)
    q = jnp.concatenate([q[..., :MLA_NOPE], rotate(q[..., MLA_NOPE:], pos, inv)], axis=-1)
    k = jnp.concatenate([k[..., :MLA_NOPE], rotate(k[..., MLA_NOPE:], pos, inv)], axis=-1)
    o = causal_attention_blocked(q, k, v)
    return o.reshape(B, S, MLA_HEADS * MLA_V)


def retention_group(rq, rk, rv, rg, pos, g_ret):
    B, S, _ = rq.shape
    inv = 1.0 / (10000.0 ** jnp.linspace(0.0, 1.0, RET_DK // 2, dtype=F32))
    q = rotate(rq.reshape(B, S, RET_HEADS, RET_DK), pos, inv)
    k = rotate(rk.reshape(B, S, RET_HEADS, RET_DK), pos, inv) * (RET_DK ** -0.5)
    v = rv.reshape(B, S, RET_HEADS, RET_DV)
    o = retention_chunkwise(q, k, v)
    o = rmsnorm(o, g_ret).astype(rv.dtype)
    return jax.nn.silu(rg) * o.reshape(B, S, RET_HEADS * RET_DV)


def setup_inputs(seed: int = 0) -> dict:
    key = jax.random.key(seed)
    ks = jax.random.split(key, 32)
    L = DEPTH

    def w(i, shape, fan_in):
        return jax.random.normal(ks[i], shape, F32) * (fan_in ** -0.5)

    def gain(i, shape):
        return 1.0 + 0.02 * jax.random.normal(ks[i], shape, F32)

    def bias(i, shape):
        return 0.02 * jax.random.normal(ks[i], shape, F32)

    x = jax.random.normal(ks[0], (BATCH, SEQ, D_MODEL), F32)
    p = jax.random.normal(ks[1], (DEPTH, BATCH, SEQ, PLE_DIM), F32)
    offset = jax.random.randint(ks[2], (BATCH, 1), 0, 1024, dtype=jnp.int32)
    positions = jnp.arange(SEQ, dtype=jnp.int32)[None, :] + offset
    return {
        "x": x, "p": p, "positions": positions,
        "g_mix": gain(3, (L, D_MODEL)),
        "w_in": w(4, (L, D_MODEL, N_IN), D_MODEL),
        "g_qa": gain(5, (L, Q_LORA)),
        "g_kva": gain(6, (L, KV_LORA)),
        "w_q_up": w(7, (L, Q_LORA, MLA_HEADS * MLA_QK), Q_LORA),
        "w_kv_up": w(8, (L, KV_LORA, MLA_HEADS * (MLA_NOPE + MLA_V)), KV_LORA),
        "g_qn": gain(9, (L, MLA_QK)),
        "g_kn": gain(10, (L, MLA_QK)),
        "w_sc": w(11, (L, SC_K, SC_WIDTH), SC_K),
        "w_cf": w(12, (L, CF_K, CF_WIDTH), CF_K),
        "b_cf": bias(13, (L, CF_WIDTH)),
        "g_cf_ln": gain(14, (L, CF_WIDTH)),
        "b_cf_ln": bias(15, (L, CF_WIDTH)),
        "g_ret": gain(16, (L, RET_HEADS, RET_DV)),
        "w_o": w(17, (L, D_MIX, D_MODEL), D_MIX),
        "g_ffn": gain(18, (L, D_MODEL)),
        "w_up": w(19, (L, D_MODEL, 2 * D_FF), D_MODEL),
        "w_ffn_conv": w(20, (L, FFN_K, 2 * D_FF), FFN_K),
        "w_down": w(21, (L, D_FF, D_MODEL), D_FF),
        "g_pe": gain(22, (L, D_MODEL)),
        "w_pe": w(23, (L, PLE_DIM, D_MODEL), PLE_DIM),
        "w_pg": w(24, (L, D_MODEL, D_MODEL), D_MODEL),
    }


def reference(x, p, positions, g_mix, w_in, g_qa, g_kva, w_q_up, w_kv_up, g_qn, g_kn,
              w_sc, w_cf, b_cf, g_cf_ln, b_cf_ln, g_ret, w_o, g_ffn, w_up, w_ffn_conv,
              w_down, g_pe, w_pe, w_pg):
    for i in range(DEPTH):
        h = rmsnorm(x, g_mix[i])
        z = h @ w_in[i]
        (zq, zkv, zkr, sc_b, sc_c, sc_h, cf_a, cf_g, rq, rk, rv, rg) = split_cols(z, IN_SIZES)
        y_a = mla_group(zq, zkv, zkr, positions, g_qa[i], g_kva[i], w_q_up[i], w_kv_up[i],
                        g_qn[i], g_kn[i])
        y_b = sc_b * causal_dwconv(sc_c * sc_h, w_sc[i])
        u = causal_dwconv(cf_a * jax.nn.sigmoid(cf_g), w_cf[i], b_cf[i])
        y_c = jax.nn.silu(layernorm(u, g_cf_ln[i], b_cf_ln[i]))
        y_d = retention_group(rq, rk, rv, rg, positions, g_ret[i])
        y = jnp.concatenate([y_a, y_b, y_c, y_d], axis=-1) @ w_o[i]
        x = x + y
        uf = causal_dwconv(rmsnorm(x, g_ffn[i]) @ w_up[i], w_ffn_conv[i])
        gate, up = jnp.split(uf, 2, axis=-1)
        x = x + (jax.nn.silu(gate) * up) @ w_down[i]
        pe_gate = jax.nn.sigmoid(rmsnorm(x, g_pe[i]) @ w_pg[i])
        x = x + (p[i] @ w_pe[i]) * pe_gate
    return x
```

```python
import math
from contextlib import ExitStack
import numpy as np
import ml_dtypes
import concourse.bass as bass
import concourse.mybir as mybir
from concourse.bass_utils import run_bass_kernel_spmd


F32 = mybir.dt.float32
BF16 = mybir.dt.bfloat16
I32 = mybir.dt.int32
AF = mybir.ActivationFunctionType
ALU = mybir.AluOpType

ENGS = ("tensor", "vector", "scalar", "gpsimd", "sync")


class BufObj:
    __slots__ = ("name", "writer", "readers")

    def __init__(self, name):
        self.name = name
        self.writer = None
        self.readers = []


class Replay:
    def __init__(self):
        self.log = []
        self.pos = 0
        self.replaying = False

    def get(self, factory):
        if self.replaying:
            o = self.log[self.pos]
            self.pos += 1
            return o
        o = factory()
        self.log.append(o)
        return o

    def restart(self):
        self.replaying = True
        self.pos = 0


CUR_REPLAY = [None]


def Buf(name):
    r = CUR_REPLAY[0]
    return BufObj(name) if r is None else r.get(lambda: BufObj(name))


class Op:
    __slots__ = ("eng", "fn", "deps", "is_dma", "sem", "seq", "signals", "idx", "dma_sem")

    def __init__(self, eng, fn, is_dma=False):
        self.eng = eng
        self.fn = fn
        self.deps = []
        self.is_dma = is_dma
        self.seq = None
        self.sem = None
        self.signals = False
        self.dma_sem = None


class Sched:
    def __init__(self, nc, same_engine_sync=True):
        self.nc = nc
        self.ops = {e: [] for e in ENGS}
        self.same_engine_sync = same_engine_sync
        self.dma_slots = {}
        self.n_dma_sems = 0

    def op(self, eng, fn, reads=(), writes=(), extra_deps=(), dma_slot=None):
        o = Op(eng, fn, is_dma=dma_slot is not None)
        deps = []
        for b in reads:
            if b.writer is not None:
                deps.append(b.writer)
        for b in writes:
            if b.writer is not None:
                deps.append(b.writer)
            deps.extend(b.readers)
        deps.extend(extra_deps)
        seen = set()
        for d in deps:
            if d is o or id(d) in seen:
                continue
            seen.add(id(d))
            o.deps.append(d)
        for b in reads:
            b.readers.append(o)
        for b in writes:
            b.writer = o
            b.readers = []
        if dma_slot is not None:
            if dma_slot not in self.dma_slots:
                self.dma_slots[dma_slot] = [self.n_dma_sems, 0]
                self.n_dma_sems += 1
            s = self.dma_slots[dma_slot]
            s[1] += 1
            o.dma_sem = (s[0], 16 * s[1])
        self.ops[eng].append(o)
        return o

    def emit(self):
        nc = self.nc
        for e in ENGS:
            for o in self.ops[e]:
                for d in o.deps:
                    if d.is_dma:
                        continue
                    if d.eng == o.eng and not self.same_engine_sync and d.eng != "sync":
                        continue
                    if d.eng == o.eng and d.eng == "tensor":
                        continue
                    d.signals = True
        for e in ENGS:
            c = 0
            for o in self.ops[e]:
                if o.is_dma:
                    continue
                if o.signals:
                    c += 1
                    o.seq = c
        with ExitStack() as st:
            esem = {e: st.enter_context(nc.semaphore("es_" + e)) for e in ENGS}
            dsem = [st.enter_context(nc.semaphore("ds_%d" % i)) for i in range(self.n_dma_sems)]
            block = st.enter_context(nc.Block())
            for e in ENGS:
                if not self.ops[e]:
                    continue
                self._emit_engine(block, e, esem, dsem)

    def _emit_engine(self, block, e, esem, dsem):
        ops = self.ops[e]
        same = self.same_engine_sync

        def body(eng):
            waited = {}
            for o in ops:
                need = {}
                for d in o.deps:
                    if d.is_dma:
                        k = ("d", d.dma_sem[0])
                        v = d.dma_sem[1]
                    else:
                        if d.eng == o.eng:
                            if d.eng == "tensor":
                                continue
                            if not same:
                                continue
                        k = ("e", d.eng)
                        v = d.seq
                    if v > need.get(k, 0):
                        need[k] = v
                for k, v in need.items():
                    if waited.get(k, 0) >= v:
                        continue
                    waited[k] = v
                    sem = dsem[k[1]] if k[0] == "d" else esem[k[1]]
                    eng.wait_ge(sem, v)
                inst = o.fn(eng)
                if o.is_dma:
                    inst.then_inc(dsem[o.dma_sem[0]], 16)
                elif o.signals:
                    inst.then_inc(esem[e], 1)

        getattr(block, e)(body)


class Ctx:
    def __init__(self, nc, st, per_shard=()):
        self.nc = nc
        self.st = st
        self.S = Sched(nc)
        self.n = 0
        self.out_dmas = []
        self.rp = Replay()
        CUR_REPLAY[0] = self.rp
        self.dram = {}
        self.per_shard = set(per_shard)
        self.suffix = ""

    def next_pass(self, suffix):
        self.rp.restart()
        self.suffix = suffix

    def sb(self, shape, dt, name=None):
        def mk():
            self.n += 1
            return self.st.enter_context(self.nc.sbuf_tensor("s_" + (name or ("t%d" % self.n)), list(shape), dt))
        return self.rp.get(mk)

    def ps(self, name=None, dt=F32, shape=(128, 512)):
        def mk():
            self.n += 1
            return self.st.enter_context(self.nc.psum_tensor("p_" + (name or ("p%d" % self.n)), list(shape), dt))
        return self.rp.get(mk)

    def _dram(self, name, shape, dt, kind):
        if name in self.per_shard:
            name = name + self.suffix
        if name not in self.dram:
            self.dram[name] = self.nc.dram_tensor(name, list(shape), dt, kind=kind).ap()
        return self.dram[name]

    def dram_in(self, name, shape, dt=F32):
        return self._dram(name, shape, dt, "ExternalInput")

    def dram_out(self, name, shape, dt=F32):
        return self._dram(name, shape, dt, "ExternalOutput")

    def finish(self):
        self.S.op("sync", lambda e: None, extra_deps=self.out_dmas)
        self.S.emit()
        CUR_REPLAY[0] = None


class Ring:
    def __init__(self, cx, n, shape, dt, name, psum=False):
        self.tiles = []
        self.bufs = []
        for i in range(n):
            t = cx.ps(name + str(i), dt, shape) if psum else cx.sb(shape, dt, name + str(i))
            self.tiles.append(t)
            self.bufs.append(Buf(name + str(i)))
        self.i = 0
        self.name = name

    def next(self):
        k = self.i % len(self.tiles)
        self.i += 1
        return self.tiles[k], self.bufs[k], self.name + str(k)


def load(cx, q, out_ap, in_ap, buf, slot):
    return cx.S.op(q, lambda e: e.dma_start(out=out_ap, in_=in_ap), writes=[buf], dma_slot=slot)


def rmsnorm_fm(cx, x_ap_fn, KC, n, g_tile, hT_ap_fn, D, eps, ones_bf, r):
    S = cx.S
    ps_t, ps_b, _ = r["ps"].next()
    sqs = []
    for k in range(KC):
        sq_t, sq_b, _ = r["sq"].next()
        S.op("scalar", lambda e, k=k, sq_t=sq_t: e.activation(out=sq_t[:, 0:n], in_=x_ap_fn(k), func=AF.Square),
             reads=[r["x_bufs"][k]], writes=[sq_b])
        S.op("tensor", lambda e, k=k, sq_t=sq_t: e.matmul(ps_t[:, 0:n], lhsT=ones_bf[:], rhs=sq_t[:, 0:n],
                                                          start=(k == 0), stop=(k == KC - 1)),
             reads=[sq_b, r["ones_buf"]], writes=[ps_b])
    rs_t, rs_b, _ = r["rstd"].next()
    S.op("scalar", lambda e: e.activation(out=rs_t[:, 0:n], in_=ps_t[:, 0:n], func=AF.Sqrt, scale=1.0 / D, bias=r["eps_ap"]),
         reads=[ps_b, r["eps_buf"]], writes=[rs_b])
    S.op("vector", lambda e: e.reciprocal(rs_t[:, 0:n], rs_t[:, 0:n]), reads=[rs_b], writes=[rs_b])
    for k in range(KC):
        S.op("vector", lambda e, k=k: e.scalar_tensor_tensor(out=hT_ap_fn(k), in0=x_ap_fn(k), scalar=g_tile[:, k:k + 1],
                                                             in1=rs_t[:, 0:n], op0=ALU.mult, op1=ALU.mult),
             reads=[r["x_bufs"][k], rs_b, r["g_buf"]], writes=[r["h_bufs"][k]])
    return rs_t, rs_b


EPS = 1e-6


def ffn_cfg(D=2048, DFF=5632, T=1024, PD=256, TB=512, JQ=11, OGW=256):
    return dict(D=D, DFF=DFF, T=T, PD=PD, TB=TB, JQ=JQ, OGW=OGW)


def build_ffn(cfg):
    D, DFF, T, PD, TB, JQ, OGW = (cfg[k] for k in ("D", "DFF", "T", "PD", "TB", "JQ", "OGW"))
    KC, NJ, PC, NB = D // 128, DFF // 128, PD // 128, T // TB
    NQ, NOG, OC = NJ // JQ, D // OGW, OGW // 128
    assert NQ * JQ == NJ
    nc = bass.Bass("TRN2", target_bir_lowering=False)
    with ExitStack() as st:
        cx = Ctx(nc, st)
        S = cx.S
        xT_d = cx.dram_in("xT", [128, KC, T])
        xh_d = cx.dram_in("xh", [128, KC, 2])
        pT_d = cx.dram_in("pT", [128, PC, T])
        gf_d = cx.dram_in("g_ffn", [128, KC])
        gp_d = cx.dram_in("g_pe", [128, KC])
        wc_d = cx.dram_in("w_conv", [128, 2 * NJ, 3])
        wup_d = cx.dram_in("w_up", [NJ, 128, KC * 256])
        wdn_d = cx.dram_in("w_down", [NQ, NOG, 128, JQ * OGW])
        wpe_d = cx.dram_in("w_pe", [NOG, 128, PC * OGW])
        wpg_d = cx.dram_in("w_pg", [NOG, 128, KC * OGW])
        out_d = cx.dram_out("xo", [128, KC, T])

        WSZ = max(KC * 256, JQ * OGW, KC * OGW)
        xT = cx.sb([128, KC, T], F32, "xT")
        xh = cx.sb([128, KC, 2], F32, "xh")
        hT = cx.sb([128, KC, 2 + T], BF16, "hT")
        pT = cx.sb([128, PC, T], BF16, "pTb")
        gf = cx.sb([128, KC], F32, "gf")
        gp = cx.sb([128, KC], F32, "gp")
        wc = cx.sb([128, 2 * NJ, 3], F32, "wc")
        ones = cx.sb([128, 128], BF16, "ones")
        epst = cx.sb([128, 1], F32, "epst")
        act = [cx.sb([128, JQ, T], BF16, "act%d" % i) for i in range(2)]
        wring = Ring(cx, 3, [128, WSZ], BF16, "wr")
        wpering = Ring(cx, 2, [128, PC * OGW], BF16, "wpe")
        hu_ring = Ring(cx, 2, [128, 2 + T], F32, "hu")
        tmp_ring = Ring(cx, 3, [128, T], F32, "tmp")
        sq_ring = Ring(cx, 3, [128, TB], BF16, "sq")
        rstd_ring = Ring(cx, 2, [128, TB], F32, "rstd")
        gate_ring = Ring(cx, 2, [128, TB], F32, "gate")
        psA = Ring(cx, 3, [128, 512], F32, "psA", psum=True)
        psHr = Ring(cx, 2, [128, 512], F32, "psH", psum=True)
        psD = Ring(cx, 2, [128, 512], F32, "psD", psum=True)
        psN = Ring(cx, 1, [128, 512], F32, "psN", psum=True)

        b_x = [[Buf("x%d_%d" % (k, b)) for b in range(NB)] for k in range(KC)]
        b_xh = Buf("xh")
        b_h = [[Buf("h%d_%d" % (k, b)) for b in range(NB)] for k in range(KC)]
        b_hh = Buf("hh")
        b_gf, b_gp, b_wc, b_ones, b_eps, b_pT = (Buf(n) for n in ("gf", "gp", "wc", "ones", "eps", "pT"))
        b_act = [[Buf("act%d_%d" % (i, jj)) for jj in range(JQ)] for i in range(2)]

        for k in range(KC):
            for b in range(NB):
                load(cx, "sync", xT[:, k, b * TB:(b + 1) * TB], xT_d[:, k, b * TB:(b + 1) * TB], b_x[k][b], "x%d_%d" % (k, b))
        load(cx, "sync", xh[:], xh_d, b_xh, "xh")
        load(cx, "sync", gf[:], gf_d, b_gf, "gf")
        load(cx, "sync", gp[:], gp_d, b_gp, "gp")
        load(cx, "sync", wc[:], wc_d, b_wc, "wc")
        load(cx, "gpsimd", pT[:], pT_d, b_pT, "pT")
        S.op("vector", lambda e: e.memset(ones[:], 1.0), writes=[b_ones])
        S.op("vector", lambda e: e.memset(epst[:], EPS), writes=[b_eps])

        wsteps = []
        order = []
        for q in range(NQ):
            order.append(("up", q))
            if q >= 1:
                order.append(("down", q - 1))
        order.append(("down", NQ - 1))
        for kind, q in order:
            if kind == "up":
                for jj in range(JQ):
                    wsteps.append(("up", q * JQ + jj, wup_d[q * JQ + jj], KC * 256))
            else:
                for og in range(NOG):
                    wsteps.append(("down", (q, og), wdn_d[q, og], JQ * OGW))
        for og in range(NOG):
            wsteps.append(("pg", og, wpg_d[og], KC * OGW))
        wslot = {}
        wpos = [0]

        def issue_wload():
            i = wpos[0]
            if i >= len(wsteps):
                return
            wpos[0] += 1
            kind, key, src, n = wsteps[i]
            t, bf, nm = wring.next()
            load(cx, "gpsimd", t[:, 0:n], src, bf, nm)
            wslot[(kind, key)] = (t, bf)

        for _ in range(3):
            issue_wload()

        def norm(xfn, n, g_tile, g_buf, hfn, xb, hb):
            r = dict(ps=psN, sq=sq_ring, rstd=rstd_ring, x_bufs=xb, h_bufs=hb, ones_buf=b_ones, g_buf=g_buf, eps_ap=epst[:, 0:1], eps_buf=b_eps)
            rmsnorm_fm(cx, xfn, KC, n, g_tile, hfn, D, EPS, ones, r)

        def norm_all(g_tile, g_buf, with_halo):
            if with_halo:
                norm(lambda k: xh[:, k, :], 2, g_tile, g_buf, lambda k: hT[:, k, 0:2], [b_xh] * KC, [b_hh] * KC)
            for b in range(NB):
                norm(lambda k, b=b: xT[:, k, b * TB:(b + 1) * TB], TB, g_tile, g_buf,
                     lambda k, b=b: hT[:, k, 2 + b * TB:2 + (b + 1) * TB],
                     [b_x[k][b] for k in range(KC)], [b_h[k][b] for k in range(KC)])

        norm_all(gf, b_gf, True)
        all_h = [b_h[k][b] for k in range(KC) for b in range(NB)]

        hcount = [0]

        def up_step(q, jj):
            j = q * JQ + jj
            wt, wb = wslot[("up", j)]
            ufs = []
            for c in range(2):
                hu_t, hu_b, _ = hu_ring.next()
                psH, psHb, _ = psHr.next()

                def mmh(e, c=c, psH=psH):
                    for k in range(KC):
                        i = e.matmul(psH[:, 0:2], lhsT=wt[:, k * 256 + c * 128:k * 256 + (c + 1) * 128],
                                     rhs=hT[:, k, 0:2], start=(k == 0), stop=(k == KC - 1))
                    return i
                S.op("tensor", mmh, reads=[wb, b_hh], writes=[psHb])
                S.op("scalar", lambda e, psH=psH, hu_t=hu_t: e.activation(out=hu_t[:, 0:2], in_=psH[:, 0:2], func=AF.Copy),
                     reads=[psHb], writes=[hu_b])
                for b in range(NB):
                    ps_t, ps_b, _ = psA.next()

                    def mm(e, c=c, b=b, ps_t=ps_t):
                        for k in range(KC):
                            i = e.matmul(ps_t[:, 0:TB], lhsT=wt[:, k * 256 + c * 128:k * 256 + (c + 1) * 128],
                                         rhs=hT[:, k, 2 + b * TB:2 + (b + 1) * TB], start=(k == 0), stop=(k == KC - 1))
                        return i
                    S.op("tensor", mm, reads=[wb] + [b_h[k][b] for k in range(KC)], writes=[ps_b])
                    S.op("scalar", lambda e, b=b, ps_t=ps_t, hu_t=hu_t: e.activation(out=hu_t[:, 2 + b * TB:2 + (b + 1) * TB], in_=ps_t[:, 0:TB], func=AF.Copy),
                         reads=[ps_b], writes=[hu_b])
                ci = c * NJ + j
                tm_t, tm_b, _ = tmp_ring.next()
                S.op("vector", lambda e, hu_t=hu_t, tm_t=tm_t, ci=ci: e.tensor_scalar(out=tm_t[:, 0:T], in0=hu_t[:, 0:T], scalar1=wc[:, ci, 0:1], scalar2=None, op0=ALU.mult),
                     reads=[hu_b, b_wc], writes=[tm_b])
                S.op("vector", lambda e, hu_t=hu_t, tm_t=tm_t, ci=ci: e.scalar_tensor_tensor(out=tm_t[:, 0:T], in0=hu_t[:, 1:T + 1], scalar=wc[:, ci, 1:2], in1=tm_t[:, 0:T], op0=ALU.mult, op1=ALU.add),
                     reads=[hu_b, b_wc, tm_b], writes=[tm_b])
                S.op("vector", lambda e, hu_t=hu_t, tm_t=tm_t, ci=ci: e.scalar_tensor_tensor(out=tm_t[:, 0:T], in0=hu_t[:, 2:T + 2], scalar=wc[:, ci, 2:3], in1=tm_t[:, 0:T], op0=ALU.mult, op1=ALU.add),
                     reads=[hu_b, b_wc, tm_b], writes=[tm_b])
                ufs.append((tm_t, tm_b))
            (g_t, g_b), (u_t, u_b) = ufs
            S.op("scalar", lambda e: e.activation(out=g_t[:, 0:T], in_=g_t[:, 0:T], func=AF.Silu), reads=[g_b], writes=[g_b])
            a_t = act[q % 2]
            S.op("vector", lambda e: e.tensor_tensor(out=a_t[:, jj, :], in0=g_t[:, 0:T], in1=u_t[:, 0:T], op=ALU.mult),
                 reads=[g_b, u_b], writes=[b_act[q % 2][jj]])
            issue_wload()

        def down_step(q):
            a_t = act[q % 2]
            for og in range(NOG):
                wt, wb = wslot[("down", (q, og))]
                for oc in range(OC):
                    ko = og * OC + oc
                    for b in range(NB):
                        ps_t, ps_b, _ = psD.next()

                        def mm(e, oc=oc, b=b, ps_t=ps_t, wt=wt):
                            for jj in range(JQ):
                                i = e.matmul(ps_t[:, 0:TB], lhsT=wt[:, jj * OGW + oc * 128:jj * OGW + (oc + 1) * 128],
                                             rhs=a_t[:, jj, b * TB:(b + 1) * TB], start=(jj == 0), stop=(jj == JQ - 1))
                            return i
                        S.op("tensor", mm, reads=[wb] + b_act[q % 2], writes=[ps_b])
                        S.op("vector", lambda e, ko=ko, b=b, ps_t=ps_t: e.tensor_tensor(out=xT[:, ko, b * TB:(b + 1) * TB], in0=ps_t[:, 0:TB], in1=xT[:, ko, b * TB:(b + 1) * TB], op=ALU.add),
                             reads=[ps_b, b_x[ko][b]], writes=[b_x[ko][b]])
                issue_wload()

        for kind, q in order:
            if kind == "up":
                for jj in range(JQ):
                    up_step(q, jj)
            else:
                down_step(q)

        norm_all(gp, b_gp, False)
        for og in range(NOG):
            wt, wb = wslot[("pg", og)]
            we_t, we_b, we_n = wpering.next()
            load(cx, "gpsimd", we_t[:], wpe_d[og], we_b, we_n)
            for oc in range(OC):
                ko = og * OC + oc
                for b in range(NB):
                    ps1, ps1b, _ = psA.next()
                    ps2, ps2b, _ = psA.next()

                    def mm1(e, oc=oc, b=b, ps1=ps1, wt=wt):
                        for k in range(KC):
                            i = e.matmul(ps1[:, 0:TB], lhsT=wt[:, k * OGW + oc * 128:k * OGW + (oc + 1) * 128],
                                         rhs=hT[:, k, 2 + b * TB:2 + (b + 1) * TB], start=(k == 0), stop=(k == KC - 1))
                        return i

                    def mm2(e, oc=oc, b=b, ps2=ps2, we_t=we_t):
                        for k in range(PC):
                            i = e.matmul(ps2[:, 0:TB], lhsT=we_t[:, k * OGW + oc * 128:k * OGW + (oc + 1) * 128],
                                         rhs=pT[:, k, b * TB:(b + 1) * TB], start=(k == 0), stop=(k == PC - 1))
                        return i
                    S.op("tensor", mm1, reads=[wb] + [b_h[k][b] for k in range(KC)], writes=[ps1b])
                    S.op("tensor", mm2, reads=[we_b, b_pT], writes=[ps2b])
                    gt, gb, _ = gate_ring.next()
                    S.op("scalar", lambda e, ps1=ps1, gt=gt: e.activation(out=gt[:, 0:TB], in_=ps1[:, 0:TB], func=AF.Sigmoid), reads=[ps1b], writes=[gb])
                    S.op("vector", lambda e, ps2=ps2, gt=gt: e.tensor_tensor(out=gt[:, 0:TB], in0=ps2[:, 0:TB], in1=gt[:, 0:TB], op=ALU.mult), reads=[ps2b, gb], writes=[gb])
                    S.op("vector", lambda e, ko=ko, b=b, gt=gt: e.tensor_tensor(out=xT[:, ko, b * TB:(b + 1) * TB], in0=xT[:, ko, b * TB:(b + 1) * TB], in1=gt[:, 0:TB], op=ALU.add),
                         reads=[gb, b_x[ko][b]], writes=[b_x[ko][b]])
                    o = S.op("sync", lambda e, ko=ko, b=b: e.dma_start(out=out_d[:, ko, b * TB:(b + 1) * TB], in_=xT[:, ko, b * TB:(b + 1) * TB]),
                             reads=[b_x[ko][b]], dma_slot="o%d_%d" % (ko, b))
                    cx.out_dmas.append(o)
            issue_wload()
        cx.finish()
    return nc


def fm(a, T=None):
    Tn, F = a.shape
    return np.ascontiguousarray(a.T.reshape(F // 128, 128, Tn).transpose(1, 0, 2))


def unfm(a):
    p, kc, T = a.shape
    return np.ascontiguousarray(a.transpose(2, 1, 0).reshape(T, kc * p))


def vec_fm(g):
    return np.ascontiguousarray(g.reshape(-1, 128).T)


def wtile(w, cols):
    K = w.shape[0]
    sub = w[:, cols]
    return np.ascontiguousarray(sub.reshape(K // 128, 128, len(cols)).transpose(1, 0, 2).reshape(128, -1))


def ffn_host_inputs(cfg, w_up, w_conv, w_down, w_pe, w_pg, g_ffn, g_pe):
    D, DFF, JQ, OGW = cfg["D"], cfg["DFF"], cfg["JQ"], cfg["OGW"]
    NJ = DFF // 128
    NQ, NOG = NJ // JQ, D // OGW
    wup = np.stack([wtile(w_up, np.concatenate([np.arange(j * 128, (j + 1) * 128), DFF + np.arange(j * 128, (j + 1) * 128)])) for j in range(NJ)])
    wdn = np.stack([np.stack([wtile(w_down[q * JQ * 128:(q + 1) * JQ * 128], np.arange(og * OGW, (og + 1) * OGW)) for og in range(NOG)]) for q in range(NQ)])
    wpe = np.stack([wtile(w_pe, np.arange(og * OGW, (og + 1) * OGW)) for og in range(NOG)])
    wpg = np.stack([wtile(w_pg, np.arange(og * OGW, (og + 1) * OGW)) for og in range(NOG)])
    wc = np.ascontiguousarray(w_conv.T.reshape(2 * NJ, 128, 3).transpose(1, 0, 2))
    return dict(w_up=wup, w_down=wdn, w_pe=wpe, w_pg=wpg, w_conv=wc, g_ffn=vec_fm(g_ffn), g_pe=vec_fm(g_pe))


def attn_cfg(S=8192, QB=512, DA=128, DB=64, DV=128, scale=192 ** -0.5):
    return dict(S=S, QB=QB, DA=DA, DB=DB, DV=DV, scale=scale)


def build_attn(cfg):
    S_, QB, DA, DB, DV, scale = (cfg[k] for k in ("S", "QB", "DA", "DB", "DV", "scale"))
    NQB = S_ // QB // 2
    KT = QB // 128
    NKT = S_ // 128
    nc = bass.Bass("TRN2", target_bir_lowering=False)
    with ExitStack() as st:
        cx = Ctx(nc, st)
        S = cx.S
        qa_d = cx.dram_in("qa", [DA, NQB, QB], BF16)
        qb_d = cx.dram_in("qb", [DB, NQB, QB], BF16)
        ka_d = cx.dram_in("ka", [DA, S_], BF16)
        kb_d = cx.dram_in("kb", [DB, S_], BF16)
        v_d = cx.dram_in("v", [128, NKT, DV], BF16)
        m_d = cx.dram_in("mask", [128, 2, KT, QB], BF16)
        o_d = cx.dram_out("o", [DV, NQB, QB], BF16)

        qa = cx.sb([DA, NQB, QB], BF16, "qa")
        qb = cx.sb([DB, NQB, QB], BF16, "qb")
        ka = cx.sb([DA, S_], BF16, "ka")
        kb = cx.sb([DB, S_], BF16, "kb")
        v = cx.sb([128, NKT, DV], BF16, "v")
        mk = cx.sb([128, 2, KT, QB], BF16, "mk")
        ones = cx.sb([128, 128], BF16, "ones")
        b_qa, b_qb, b_m, b_ones = Buf("qa"), Buf("qb"), Buf("m"), Buf("ones")
        NKG = max(1, NKT // 16)
        GK = NKT // NKG
        b_k = [Buf("k%d" % i) for i in range(NKG)]
        b_v = [Buf("v%d" % i) for i in range(NKG)]
        load(cx, "sync", qa[:], qa_d, b_qa, "qa")
        load(cx, "sync", qb[:], qb_d, b_qb, "qb")
        load(cx, "sync", mk[:], m_d, b_m, "mk")
        S.op("vector", lambda e: e.memset(ones[:], 1.0), writes=[b_ones])
        for g in range(NKG):
            c0, c1 = g * GK * 128, (g + 1) * GK * 128
            S.op("sync", lambda e, c0=c0, c1=c1: e.dma_start(out=ka[:, c0:c1], in_=ka_d[:, c0:c1]), writes=[b_k[g]], dma_slot="ka%d" % g)
            S.op("sync", lambda e, c0=c0, c1=c1: e.dma_start(out=kb[:, c0:c1], in_=kb_d[:, c0:c1]), writes=[b_k[g]], dma_slot="kb%d" % g)
            S.op("sync", lambda e, g=g: e.dma_start(out=v[:, g * GK:(g + 1) * GK, :], in_=v_d[:, g * GK:(g + 1) * GK, :]), writes=[b_v[g]], dma_slot="v%d" % g)

        psS = Ring(cx, 3, [128, 512], F32, "psS", psum=True)
        psO = Ring(cx, 2, [128, 512], F32, "psO", psum=True)
        psL = Ring(cx, 2, [128, 512], F32, "psL", psum=True)
        pt_ring = Ring(cx, 4, [128, QB], BF16, "pt")
        rec_ring = Ring(cx, 2, [128, QB], F32, "rec")
        o_ring = Ring(cx, 2, [DV, QB], BF16, "osb")

        def do_block(j):
            tiles = [(i, kt) for i in range(2 * j + 2) for kt in range(KT)]
            o_t, o_b, _ = psO.next()
            l_t, l_b, _ = psL.next()
            pend = []

            def scores(i, kt):
                g = i * KT + kt
                s_t, s_b, _ = psS.next()

                def mm(e, g=g, s_t=s_t):
                    e.matmul(s_t[:, 0:QB], lhsT=ka[:, g * 128:(g + 1) * 128], rhs=qa[:, j, :], start=True, stop=False)
                    return e.matmul(s_t[:, 0:QB], lhsT=kb[:, g * 128:(g + 1) * 128], rhs=qb[:, j, :], start=False, stop=True)
                S.op("tensor", mm, reads=[b_k[g // GK], b_qa, b_qb], writes=[s_b])
                p_t, p_b, _ = pt_ring.next()
                S.op("scalar", lambda e, s_t=s_t, p_t=p_t: e.activation(out=p_t[:, 0:QB], in_=s_t[:, 0:QB], func=AF.Exp, scale=scale),
                     reads=[s_b], writes=[p_b])
                if i >= 2 * j:
                    S.op("vector", lambda e, p_t=p_t, m=i - 2 * j, kt=kt: e.tensor_tensor(out=p_t[:, 0:QB], in0=p_t[:, 0:QB], in1=mk[:, m, kt, :], op=ALU.mult),
                         reads=[p_b, b_m], writes=[p_b])
                return (g, p_t, p_b)

            def pv(idx, g, p_t, p_b):
                first, last = idx == 0, idx == len(tiles) - 1
                S.op("tensor", lambda e: e.matmul(o_t[0:DV, 0:QB], lhsT=v[:, g, :], rhs=p_t[:, 0:QB], start=first, stop=last),
                     reads=[b_v[g // GK], p_b], writes=[o_b])
                S.op("tensor", lambda e: e.matmul(l_t[:, 0:QB], lhsT=ones[:], rhs=p_t[:, 0:QB], start=first, stop=last),
                     reads=[b_ones, p_b], writes=[l_b])

            LOOK = 2
            for idx, (i, kt) in enumerate(tiles):
                pend.append(scores(i, kt))
                if idx >= LOOK:
                    pv(idx - LOOK, *pend[idx - LOOK])
            for idx in range(max(0, len(tiles) - LOOK), len(tiles)):
                pv(idx, *pend[idx])
            r_t, r_b, _ = rec_ring.next()
            S.op("vector", lambda e, r_t=r_t, l_t=l_t: e.reciprocal(r_t[:, 0:QB], l_t[:, 0:QB]), reads=[l_b], writes=[r_b])
            os_t, os_b, os_n = o_ring.next()
            S.op("vector", lambda e, r_t=r_t, o_t=o_t, os_t=os_t: e.tensor_tensor(out=os_t[:, 0:QB], in0=o_t[0:DV, 0:QB], in1=r_t[0:DV, 0:QB], op=ALU.mult),
                 reads=[o_b, r_b], writes=[os_b])
            o = S.op("sync", lambda e, os_t=os_t, j=j: e.dma_start(out=o_d[:, j, :], in_=os_t[:, 0:QB]), reads=[os_b], dma_slot=os_n)
            cx.out_dmas.append(o)

        for j in range(NQB):
            do_block(j)
        cx.finish()
    return nc


def attn_masks(cfg, parity):
    QB = cfg["QB"]
    KT = QB // 128
    k = np.arange(128)[:, None, None] + 128 * np.arange(KT)[None, :, None]
    q = np.arange(QB)[None, None, :]
    tri = (k <= q).astype(np.float32)
    ones = np.ones_like(tri)
    zeros = np.zeros_like(tri)
    m = np.stack([tri, zeros], 1) if parity == 0 else np.stack([ones, tri], 1)
    return m


EPS = 1e-6
HALO = 32
OFF = dict(zq=0, zkv=512, zkr=768, sc_b=832, sc_c=1344, sc_h=1856, cf_a=2368, cf_g=2880, rq=3392, rk=3648, rv=3904, rg=4416)
TWO_PI = 2.0 * math.pi
C1 = 6.28125
C2 = float(np.float32(TWO_PI - 6.28125))
C3 = float(TWO_PI - 6.28125 - float(np.float32(TWO_PI - 6.28125)))


def mix_cfg(D=2048, T=512, TB=512, NCORE=16, NLOC=2):
    return dict(D=D, T=T, TB=TB, NCORE=NCORE, NLOC=NLOC)


def build_mix(cfg, mode):
    D, T, TB, NCORE = cfg["D"], cfg["T"], cfg["TB"], cfg["NCORE"]
    KC, NB, NT = D // 128, T // TB, T // 128
    assert NB == 1
    ND = NCORE - 1
    A = mode == "A"
    nc = bass.Bass("TRN2", target_bir_lowering=False)
    PER = ("xT", "pos", "qa_o", "qb_o", "ka_o", "kb_o", "v_o", "s_o", "xh", "o_attn", "s_prev", "xo")
    with ExitStack() as st:
        cx = Ctx(nc, st, per_shard=PER)
        for sh in range(cfg.get("NLOC", 1)):
            if sh > 0:
                cx.next_pass("_%d" % sh)
            else:
                cx.suffix = "_0"
            _mix_body(cx, cfg, mode, A, D, T, TB, NCORE, KC, NB, NT, ND)
        cx.finish()
    return nc


def _mix_body(cx, cfg, mode, A, D, T, TB, NCORE, KC, NB, NT, ND):
    if True:
        S = cx.S
        xT_d = cx.dram_in("xT", [128, KC, T])
        pos_d = cx.dram_in("pos", [1, T], I32)
        gm_d = cx.dram_in("g_mix", [128, KC])
        cst_d = cx.dram_in("cst", [128, 8])
        pm_d = cx.dram_in("pmat", [128, 128])
        kdec_d = cx.dram_in("kdec", [128, NT if A else 1, 4])
        w_rk_d = cx.dram_in("w_rk", [128, KC * 256])
        w_rv_d = cx.dram_in("w_rv", [128, KC * 512])
        idb_d = cx.dram_in("ident", [128, 128])
        if A:
            w_zq_d = cx.dram_in("w_zq", [128, KC * 512])
            w_zkv_d = cx.dram_in("w_zkv", [128, KC * 320])
            gqa_d = cx.dram_in("g_qa", [128, 4])
            gkva_d = cx.dram_in("g_kva", [128, 2])
            wqu_d = cx.dram_in("w_q_up", [128, 4 * 768])
            wkvu_d = cx.dram_in("w_kv_up", [128, 2 * 1024])
            gqn_d = cx.dram_in("g_qn", [128, 2])
            gkn_d = cx.dram_in("g_kn", [128, 2])
            qa_o = cx.dram_out("qa_o", [128, 4, T], BF16)
            qb_o = cx.dram_out("qb_o", [64, 4, T], BF16)
            ka_o = cx.dram_out("ka_o", [128, 4, T], BF16)
            kb_o = cx.dram_out("kb_o", [64, 4, T], BF16)
            v_o = cx.dram_out("v_o", [128, NT, 512], BF16)
            s_o = cx.dram_out("s_o", [64, 4, 128])
        else:
            xh_d = cx.dram_in("xh", [128, KC, HALO])
            w_sc_d = cx.dram_in("w_scp", [3, 128, KC * 512])
            w_cf_d = cx.dram_in("w_cfp", [2, 128, KC * 512])
            w_rq_d = cx.dram_in("w_rq", [128, KC * 256])
            w_rg_d = cx.dram_in("w_rg", [128, KC * 512])
            wsc_d = cx.dram_in("w_sc", [128, 4, 3])
            wcf_d = cx.dram_in("w_cf", [128, 4, 31])
            cfv_d = cx.dram_in("cf_vec", [128, 4, 3])
            gret_d = cx.dram_in("g_ret", [128, 4])
            oat_d = cx.dram_in("o_attn", [128, 4, T], BF16)
            sprev_d = cx.dram_in("s_prev", [64, ND, 4, 128])
            rdec_d = cx.dram_in("rdec", [64, 4, ND + 1])
            qdec_d = cx.dram_in("qdec", [64, 4, 128])
            dmask_d = cx.dram_in("dmask", [128, 4, 128])
            w_o_d = cx.dram_in("w_o", [KC * 128 // 256, 128, 16 * 256])
            xo_d = cx.dram_out("xo", [128, KC, T])

        HO = 0 if A else HALO
        xT = cx.sb([128, KC, T], F32, "xT")
        hT = cx.sb([128, KC, HO + T], BF16, "hT")
        gm = cx.sb([128, KC], F32, "gm")
        cst = cx.sb([128, 8], F32, "cst")
        pmat = cx.sb([128, 128], F32, "pmat")
        ident = cx.sb([128, 128], BF16, "ident")
        ones = cx.sb([128, 128], BF16, "ones")
        epst = cx.sb([128, 1], F32, "epst")
        kdec = cx.sb([128, NT if A else 1, 4], F32, "kdec")
        posi = cx.sb([128, T], I32, "posi")
        posf = cx.sb([128, T], F32, "posf")
        cos_r = cx.sb([128, T], F32, "cos_r")
        sin_r = cx.sb([128, T], F32, "sin_r")
        b_x = [[Buf("x") for b in range(NB)] for k in range(KC)]
        b_h = [[Buf("h") for b in range(NB)] for k in range(KC)]
        b_gm, b_cst, b_pm, b_id, b_ones, b_eps, b_kdec, b_pos = (Buf(n) for n in ("gm", "cst", "pm", "id", "ones", "eps", "kdec", "pos"))
        b_cr, b_sr = Buf("cos_r"), Buf("sin_r")
        for k in range(KC):
            for b in range(NB):
                load(cx, "sync", xT[:, k, b * TB:(b + 1) * TB], xT_d[:, k, b * TB:(b + 1) * TB], b_x[k][b], "x%d_%d" % (k, b))
        load(cx, "sync", gm[:], gm_d, b_gm, "gm")
        load(cx, "sync", cst[:], cst_d, b_cst, "cst")
        load(cx, "sync", pmat[:], pm_d, b_pm, "pm")
        load(cx, "sync", kdec[:], kdec_d, b_kdec, "kdec")
        load(cx, "gpsimd", ident[:], idb_d, b_id, "ident")
        load(cx, "sync", posi[:], pos_d[0].partition_broadcast(128), b_pos, "pos")
        S.op("vector", lambda e: e.memset(ones[:], 1.0), writes=[b_ones])
        S.op("vector", lambda e: e.memset(epst[:], EPS), writes=[b_eps])

        psA = Ring(cx, 4, [128, 512], F32, "psA", psum=True)
        psN = Ring(cx, 2, [128, 512], F32, "psN", psum=True)
        psT = Ring(cx, 2, [128, 512], BF16, "psT", psum=True)
        sq_ring = Ring(cx, 3, [128, max(TB, 512)], BF16, "sq")
        rstd_ring = Ring(cx, 2, [128, max(TB, 512)], F32, "rstd")
        f32_ring = Ring(cx, 4, [128, TB], F32, "f32r")
        wring = Ring(cx, 2, [128, max(KC * 512, 16 * 256)], BF16, "wr")

        def wload(src, n):
            t, bf, nm = wring.next()
            load(cx, "gpsimd", t[:, 0:n], src, bf, nm)
            return t, bf

        ang_t = cx.sb([128, T], F32, "ang_t")
        kf_t = cx.sb([128, T], F32, "kf_t")
        b_ang, b_kf = Buf("ang"), Buf("kf")

        def sin_shift(npart, out_t, b_out, frac, rad):
            P_ = slice(0, npart)
            S.op("vector", lambda e: e.tensor_scalar(out=out_t[P_, :], in0=ang_t[P_, :], scalar1=1.0 / TWO_PI, scalar2=frac, op0=ALU.mult, op1=ALU.add),
                 reads=[b_ang], writes=[b_out])
            S.op("vector", lambda e: e.tensor_copy(posi[P_, :], out_t[P_, :]), reads=[b_out], writes=[b_pos])
            S.op("vector", lambda e: e.tensor_copy(kf_t[P_, :], posi[P_, :]), reads=[b_pos], writes=[b_kf])
            mk_t, mk_b, _ = f32_ring.next()
            S.op("vector", lambda e: e.tensor_tensor(out=mk_t[P_, 0:T], in0=kf_t[P_, :], in1=out_t[P_, :], op=ALU.is_gt), reads=[b_kf, b_out], writes=[mk_b])
            S.op("vector", lambda e: e.tensor_tensor(out=kf_t[P_, :], in0=kf_t[P_, :], in1=mk_t[P_, 0:T], op=ALU.subtract), reads=[b_kf, mk_b], writes=[b_kf])
            S.op("vector", lambda e: e.scalar_tensor_tensor(out=out_t[P_, :], in0=kf_t[P_, :], scalar=-C1, in1=ang_t[P_, :], op0=ALU.mult, op1=ALU.add),
                 reads=[b_kf, b_ang, b_out], writes=[b_out])
            for cc in (C2, C3):
                S.op("vector", lambda e, cc=cc: e.scalar_tensor_tensor(out=out_t[P_, :], in0=kf_t[P_, :], scalar=-cc, in1=out_t[P_, :], op0=ALU.mult, op1=ALU.add),
                     reads=[b_kf, b_out], writes=[b_out])
            S.op("vector", lambda e: e.tensor_scalar(out=out_t[P_, :], in0=out_t[P_, :], scalar1=rad - math.pi, scalar2=-math.pi, op0=ALU.add, op1=ALU.max), reads=[b_out], writes=[b_out])
            S.op("vector", lambda e: e.tensor_scalar(out=out_t[P_, :], in0=out_t[P_, :], scalar1=math.pi, scalar2=None, op0=ALU.min), reads=[b_out], writes=[b_out])
            S.op("scalar", lambda e: e.activation(out=out_t[P_, :], in_=out_t[P_, :], func=AF.Sin, scale=-1.0), reads=[b_out], writes=[b_out])

        def rope_table(col, npart, cos_t, sin_t, b_c, b_s):
            S.op("vector", lambda e: e.tensor_scalar(out=ang_t[0:npart, :], in0=posf[0:npart, :], scalar1=cst[0:npart, col:col + 1], scalar2=None, op0=ALU.mult),
                 reads=[b_pos, b_cst, b_c, b_s], writes=[b_ang])
            sin_shift(npart, sin_t, b_s, 0.0, 0.0)
            sin_shift(npart, cos_t, b_c, 0.25, math.pi / 2)

        def rope_apply(src_fn, npart, n, c_fn, s_fn, rb, out_fn, src_bufs, out_bufs, scale=None):
            ps_t, ps_b, _ = psA.next()
            S.op("tensor", lambda e: e.matmul(ps_t[0:npart, 0:n], lhsT=pmat[0:npart, 0:npart], rhs=src_fn(), start=True, stop=True),
                 reads=src_bufs + [b_pm], writes=[ps_b])
            t1, t1b, _ = f32_ring.next()
            S.op("vector", lambda e: e.tensor_tensor(out=t1[0:npart, 0:n], in0=ps_t[0:npart, 0:n], in1=s_fn(), op=ALU.mult), reads=[ps_b] + rb, writes=[t1b])
            t2, t2b, _ = f32_ring.next()
            S.op("vector", lambda e: e.tensor_tensor(out=t2[0:npart, 0:n], in0=src_fn(), in1=c_fn(), op=ALU.mult), reads=src_bufs + rb, writes=[t2b])
            if scale is None:
                S.op("vector", lambda e: e.tensor_tensor(out=out_fn(), in0=t1[0:npart, 0:n], in1=t2[0:npart, 0:n], op=ALU.add), reads=[t1b, t2b], writes=out_bufs)
            else:
                S.op("vector", lambda e: e.tensor_tensor(out=t1[0:npart, 0:n], in0=t1[0:npart, 0:n], in1=t2[0:npart, 0:n], op=ALU.add), reads=[t1b, t2b], writes=[t1b])
                S.op("vector", lambda e: e.tensor_scalar(out=out_fn(), in0=t1[0:npart, 0:n], scalar1=scale, scalar2=None, op0=ALU.mult), reads=[t1b], writes=out_bufs)

        S.op("vector", lambda e: e.tensor_copy(posf[:], posi[:]), reads=[b_pos], writes=[b_pos])
        rope_table(1, 64, cos_r, sin_r, b_cr, b_sr)
        if A:
            cos_m = cx.sb([64, T], F32, "cos_m")
            sin_m = cx.sb([64, T], F32, "sin_m")
            b_cm, b_sm = Buf("cos_m"), Buf("sin_m")
            rope_table(0, 64, cos_m, sin_m, b_cm, b_sm)

        def norm(xfn, n, hfn, xb, hb):
            r = dict(ps=psN, sq=sq_ring, rstd=rstd_ring, x_bufs=xb, h_bufs=hb, ones_buf=b_ones, g_buf=b_gm, eps_ap=epst[:, 0:1], eps_buf=b_eps)
            rmsnorm_fm(cx, xfn, KC, n, gm, hfn, D, EPS, ones, r)

        if not A:
            xh = cx.sb([128, KC, HALO], F32, "xh")
            b_xh, b_hh = Buf("xh"), Buf("hh")
            load(cx, "sync", xh[:], xh_d, b_xh, "xh")
            norm(lambda k: xh[:, k, :], HALO, lambda k: hT[:, k, 0:HALO], [b_xh] * KC, [b_hh] * KC)
        for b in range(NB):
            norm(lambda k, b=b: xT[:, k, b * TB:(b + 1) * TB], TB, lambda k, b=b: hT[:, k, HO + b * TB:HO + (b + 1) * TB],
                 [b_x[k][b] for k in range(KC)], [b_h[k][b] for k in range(KC)])

        def proj_fm(wt, wb, wstride, c0, m, b, ps_t, halo=False):
            if halo:
                rhs_fn, n, rb = (lambda k: hT[:, k, 0:HALO]), HALO, [b_hh]
            else:
                rhs_fn, n, rb = (lambda k: hT[:, k, HO + b * TB:HO + (b + 1) * TB]), TB, [b_h[k][b] for k in range(KC)]

            def mm(e):
                for k in range(KC):
                    i = e.matmul(ps_t[0:m, 0:n], lhsT=wt[:, k * wstride + c0:k * wstride + c0 + m], rhs=rhs_fn(k), start=(k == 0), stop=(k == KC - 1))
                return i
            return mm, [wb] + rb, n

        rkTb = cx.sb([64, 4, T], BF16, "rkTb")
        kd = cx.sb([128, NT, 256], BF16, "kd")
        rv = cx.sb([128, NT, 512], BF16, "rv")
        b_rkTb = [Buf("rkTb%d" % h) for h in range(4)]
        b_kd = [Buf("kd") for n in range(NT)]
        b_rv = [Buf("rv") for n in range(NT)]
        wt, wb = wload(w_rk_d, KC * 256)
        for h in range(4):
            ps_t, ps_b, _ = psA.next()
            mm, rd, n = proj_fm(wt, wb, 256, h * 64, 64, 0, ps_t)
            S.op("tensor", mm, reads=rd, writes=[ps_b])
            raw, rawb, _ = f32_ring.next()
            S.op("scalar", lambda e, raw=raw, ps_t=ps_t: e.activation(out=raw[0:64, 0:TB], in_=ps_t[0:64, 0:TB], func=AF.Copy), reads=[ps_b], writes=[rawb])
            rope_apply(lambda raw=raw: raw[0:64, 0:TB], 64, TB, lambda: cos_r[0:64, 0:TB], lambda: sin_r[0:64, 0:TB],
                       [b_cr, b_sr], lambda h=h: rkTb[:, h, 0:TB], [rawb], [b_rkTb[h]], scale=0.125)
        for n_ in range(NT):
            pt_t, pt_b, _ = psT.next()
            for h in range(4):
                S.op("tensor", lambda e, h=h, n_=n_, pt_t=pt_t: e.transpose(pt_t[:, h * 64:(h + 1) * 64], rkTb[:, h, n_ * 128:(n_ + 1) * 128], ident[0:64, 0:64]),
                     reads=[b_rkTb[h], b_id], writes=[pt_b])
            for h in range(4):
                S.op("vector", lambda e, h=h, n_=n_, pt_t=pt_t: e.tensor_scalar(out=kd[:, n_, h * 64:(h + 1) * 64], in0=pt_t[:, h * 64:(h + 1) * 64],
                                                                            scalar1=kdec[:, (n_ if A else 0), h:h + 1], scalar2=None, op0=ALU.mult),
                     reads=[pt_b, b_kdec], writes=[b_kd[n_]])
        wt, wb = wload(w_rv_d, KC * 512)
        for n_ in range(NT):
            b = (n_ * 128) // TB
            ps_t, ps_b, _ = psA.next()

            def mmv(e, n_=n_, ps_t=ps_t, wt=wt):
                for k in range(KC):
                    i = e.matmul(ps_t[:, 0:512], lhsT=hT[:, k, HO + n_ * 128:HO + (n_ + 1) * 128], rhs=wt[:, k * 512:(k + 1) * 512], start=(k == 0), stop=(k == KC - 1))
                return i
            S.op("tensor", mmv, reads=[wb] + [b_h[k][b] for k in range(KC)], writes=[ps_b])
            S.op("scalar", lambda e, n_=n_, ps_t=ps_t: e.activation(out=rv[:, n_, :], in_=ps_t[:, 0:512], func=AF.Copy), reads=[ps_b], writes=[b_rv[n_]])

        if A:
            build_A(cx, locals())
        else:
            build_B(cx, locals())


def build_A(cx, L):
    S = cx.S
    (T, TB, NB, NT, KC) = (L[k] for k in ("T", "TB", "NB", "NT", "KC"))
    (psA, psN, f32_ring, sq_ring, rstd_ring, hT, b_h, ones, b_ones, epst, b_eps, kd, rv, b_kd, b_rv, wload, proj_fm, rope_apply) = (
        L[k] for k in ("psA", "psN", "f32_ring", "sq_ring", "rstd_ring", "hT", "b_h", "ones", "b_ones", "epst", "b_eps", "kd", "rv", "b_kd", "b_rv", "wload", "proj_fm", "rope_apply"))
    cos_m, sin_m, b_cm, b_sm = L["cos_m"], L["sin_m"], L["b_cm"], L["b_sm"]
    sst = cx.sb([64, 4, 128], F32, "sst")
    b_sst = Buf("sst")
    ps_t, ps_b, _ = psA.next()
    for h in range(4):
        def mms(e, h=h, ps_t=ps_t):
            for n_ in range(NT):
                i = e.matmul(ps_t[0:64, h * 128:(h + 1) * 128], lhsT=kd[:, n_, h * 64:(h + 1) * 64], rhs=rv[:, n_, h * 128:(h + 1) * 128],
                             start=(n_ == 0), stop=(n_ == NT - 1))
            return i
        S.op("tensor", mms, reads=b_kd + b_rv, writes=[ps_b])
    S.op("vector", lambda e, ps_t=ps_t: e.tensor_copy(sst[:], ps_t[0:64, 0:512]), reads=[ps_b], writes=[b_sst])
    cx.out_dmas.append(S.op("sync", lambda e: e.dma_start(out=L["s_o"], in_=sst[:]), reads=[b_sst], dma_slot="s_o"))

    gqa = cx.sb([128, 4], F32, "gqa"); gkva = cx.sb([128, 2], F32, "gkva")
    gqn = cx.sb([128, 2], F32, "gqn"); gkn = cx.sb([128, 2], F32, "gkn")
    b_g = Buf("mla_g")
    for t_, d_, nm in ((gqa, L["gqa_d"], "gqa"), (gkva, L["gkva_d"], "gkva"), (gqn, L["gqn_d"], "gqn"), (gkn, L["gkn_d"], "gkn")):
        S.op("sync", lambda e, t_=t_, d_=d_: e.dma_start(out=t_[:], in_=d_), writes=[b_g], dma_slot=nm)
    wqu = cx.sb([128, 4 * 768], BF16, "wqu"); wkvu = cx.sb([128, 2 * 1024], BF16, "wkvu")
    b_wqu, b_wkvu = Buf("wqu"), Buf("wkvu")
    load(cx, "gpsimd", wqu[:], L["wqu_d"], b_wqu, "wqu")
    load(cx, "gpsimd", wkvu[:], L["wkvu_d"], b_wkvu, "wkvu")
    zq = cx.sb([128, 4, TB], F32, "zq"); zkv = cx.sb([128, 2, TB], F32, "zkv"); zkr = cx.sb([64, TB], F32, "zkr")
    cq = cx.sb([128, 4, TB], BF16, "cq"); ckv = cx.sb([128, 2, TB], BF16, "ckv")
    ckvT_bufs = None
    qn = cx.sb([128, TB], F32, "qn"); qr = cx.sb([64, TB], F32, "qr")
    krb = cx.sb([64, TB], F32, "krb")
    zkrg = cx.sb([64, TB], F32, "zkrg")
    b_zkrg = Buf("zkrg")
    qa_s = cx.sb([128, 4, T], BF16, "qa_s"); qb_s = cx.sb([64, 4, T], BF16, "qb_s")
    ka_s = cx.sb([128, 4, T], BF16, "ka_s"); kb_s = cx.sb([64, 4, T], BF16, "kb_s")
    v_s = cx.sb([128, NT, 512], BF16, "v_s")
    b_zq, b_zkv, b_zkr, b_cq, b_ckv, b_qn, b_qr, b_krb = (Buf(n) for n in ("zq", "zkv", "zkr", "cq", "ckv", "qn", "qr", "krb"))
    b_out = [Buf("o%d" % i) for i in range(4)]
    b_vs = [Buf("vs") for n in range(NT)]
    wzq, b_wzq = wload(L["w_zq_d"], KC * 512)
    wzkv, b_wzkv = wload(L["w_zkv_d"], KC * 320)

    def ssq_rstd(srcs, nfeat, n):
        ps_t, ps_b, _ = psN.next()
        for i, (fn, npart, bufs) in enumerate(srcs):
            sq_t, sq_b, _ = sq_ring.next()
            S.op("scalar", lambda e, fn=fn, npart=npart, sq_t=sq_t: e.activation(out=sq_t[0:npart, 0:n], in_=fn(), func=AF.Square), reads=bufs, writes=[sq_b])
            S.op("tensor", lambda e, i=i, npart=npart, sq_t=sq_t: e.matmul(ps_t[:, 0:n], lhsT=ones[0:npart, :], rhs=sq_t[0:npart, 0:n], start=(i == 0), stop=(i == len(srcs) - 1)),
                 reads=[sq_b, b_ones], writes=[ps_b])
        rs_t, rs_b, _ = rstd_ring.next()
        S.op("scalar", lambda e: e.activation(out=rs_t[:, 0:n], in_=ps_t[:, 0:n], func=AF.Sqrt, scale=1.0 / nfeat, bias=epst[:, 0:1]), reads=[ps_b, b_eps], writes=[rs_b])
        S.op("vector", lambda e: e.reciprocal(rs_t[:, 0:n], rs_t[:, 0:n]), reads=[rs_b], writes=[rs_b])
        return rs_t, rs_b

    for b in range(NB):
        tsl = slice(b * TB, (b + 1) * TB)
        for oc in range(4):
            ps_t, ps_b, _ = psA.next()
            mm, rd, n = proj_fm(wzq, b_wzq, 512, oc * 128, 128, b, ps_t)
            S.op("tensor", mm, reads=rd, writes=[ps_b])
            S.op("scalar", lambda e, oc=oc, ps_t=ps_t: e.activation(out=zq[:, oc, :], in_=ps_t[:, 0:TB], func=AF.Copy), reads=[ps_b], writes=[b_zq])
        for oc in range(2):
            ps_t, ps_b, _ = psA.next()
            mm, rd, n = proj_fm(wzkv, b_wzkv, 320, oc * 128, 128, b, ps_t)
            S.op("tensor", mm, reads=rd, writes=[ps_b])
            S.op("scalar", lambda e, oc=oc, ps_t=ps_t: e.activation(out=zkv[:, oc, :], in_=ps_t[:, 0:TB], func=AF.Copy), reads=[ps_b], writes=[b_zkv])
        ps_t, ps_b, _ = psA.next()
        mm, rd, n = proj_fm(wzkv, b_wzkv, 320, 256, 64, b, ps_t)
        S.op("tensor", mm, reads=rd, writes=[ps_b])
        S.op("scalar", lambda e, ps_t=ps_t: e.activation(out=zkr[:, :], in_=ps_t[0:64, 0:TB], func=AF.Copy), reads=[ps_b], writes=[b_zkr])
        rs_t, rs_b = ssq_rstd([(lambda oc=oc: zq[:, oc, :], 128, [b_zq]) for oc in range(4)], 512, TB)
        for oc in range(4):
            S.op("vector", lambda e, oc=oc, rs_t=rs_t: e.scalar_tensor_tensor(out=cq[:, oc, :], in0=zq[:, oc, :], scalar=gqa[:, oc:oc + 1], in1=rs_t[:, 0:TB], op0=ALU.mult, op1=ALU.mult),
                 reads=[b_zq, rs_b, b_g], writes=[b_cq])
        rs_t, rs_b = ssq_rstd([(lambda oc=oc: zkv[:, oc, :], 128, [b_zkv]) for oc in range(2)], 256, TB)
        for oc in range(2):
            S.op("vector", lambda e, oc=oc, rs_t=rs_t: e.scalar_tensor_tensor(out=ckv[:, oc, :], in0=zkv[:, oc, :], scalar=gkva[:, oc:oc + 1], in1=rs_t[:, 0:TB], op0=ALU.mult, op1=ALU.mult),
                 reads=[b_zkv, rs_b, b_g], writes=[b_ckv])
        S.op("vector", lambda e: e.tensor_scalar(out=zkrg[:, :], in0=zkr[:, :], scalar1=gkn[0:64, 1:2], scalar2=None, op0=ALU.mult), reads=[b_zkr, b_g], writes=[b_zkrg])
        rope_apply(lambda: zkrg[:, :], 64, TB, lambda tsl=tsl: cos_m[:, tsl], lambda tsl=tsl: sin_m[:, tsl], [b_cm, b_sm], lambda: krb[:, :], [b_zkrg], [b_krb])
        for h in range(4):
            psn, psnb, _ = psA.next()
            psr, psrb, _ = psA.next()

            def mmq(e, h=h, psn=psn, psr=psr):
                for k in range(4):
                    e.matmul(psn[:, 0:TB], lhsT=wqu[:, k * 768 + h * 192:k * 768 + h * 192 + 128], rhs=cq[:, k, :], start=(k == 0), stop=(k == 3))
                for k in range(4):
                    i = e.matmul(psr[0:64, 0:TB], lhsT=wqu[:, k * 768 + h * 192 + 128:k * 768 + h * 192 + 192], rhs=cq[:, k, :], start=(k == 0), stop=(k == 3))
                return i
            S.op("tensor", mmq, reads=[b_wqu, b_cq], writes=[psnb, psrb])
            S.op("scalar", lambda e, psn=psn: e.activation(out=qn[:, :], in_=psn[:, 0:TB], func=AF.Copy), reads=[psnb], writes=[b_qn])
            S.op("scalar", lambda e, psr=psr: e.activation(out=qr[:, :], in_=psr[0:64, 0:TB], func=AF.Copy), reads=[psrb], writes=[b_qr])
            rs_t, rs_b = ssq_rstd([(lambda: qn[:, :], 128, [b_qn]), (lambda: qr[:, :], 64, [b_qr])], 192, TB)
            S.op("vector", lambda e, h=h, rs_t=rs_t, tsl=tsl: e.scalar_tensor_tensor(out=qa_s[:, h, tsl], in0=qn[:, :], scalar=gqn[:, 0:1], in1=rs_t[:, 0:TB], op0=ALU.mult, op1=ALU.mult),
                 reads=[b_qn, rs_b, b_g], writes=[b_out[0]])
            S.op("vector", lambda e, rs_t=rs_t: e.scalar_tensor_tensor(out=qr[:, :], in0=qr[:, :], scalar=gqn[0:64, 1:2], in1=rs_t[0:64, 0:TB], op0=ALU.mult, op1=ALU.mult),
                 reads=[b_qr, rs_b, b_g], writes=[b_qr])
            rope_apply(lambda: qr[:, :], 64, TB, lambda tsl=tsl: cos_m[:, tsl], lambda tsl=tsl: sin_m[:, tsl], [b_cm, b_sm],
                       lambda h=h, tsl=tsl: qb_s[:, h, tsl], [b_qr], [b_out[1]])
            psk, pskb, _ = psA.next()

            def mmk(e, h=h, psk=psk):
                for k in range(2):
                    i = e.matmul(psk[:, 0:TB], lhsT=wkvu[:, k * 1024 + h * 128:k * 1024 + (h + 1) * 128], rhs=ckv[:, k, :], start=(k == 0), stop=(k == 1))
                return i
            S.op("tensor", mmk, reads=[b_wkvu, b_ckv], writes=[pskb])
            S.op("scalar", lambda e, psk=psk: e.activation(out=qn[:, :], in_=psk[:, 0:TB], func=AF.Copy), reads=[pskb], writes=[b_qn])
            rs_t, rs_b = ssq_rstd([(lambda: qn[:, :], 128, [b_qn]), (lambda: zkr[:, :], 64, [b_zkr])], 192, TB)
            S.op("vector", lambda e, h=h, rs_t=rs_t, tsl=tsl: e.scalar_tensor_tensor(out=ka_s[:, h, tsl], in0=qn[:, :], scalar=gkn[:, 0:1], in1=rs_t[:, 0:TB], op0=ALU.mult, op1=ALU.mult),
                 reads=[b_qn, rs_b, b_g], writes=[b_out[2]])
            S.op("vector", lambda e, h=h, rs_t=rs_t, tsl=tsl: e.tensor_tensor(out=kb_s[:, h, tsl], in0=krb[:, :], in1=rs_t[0:64, 0:TB], op=ALU.mult),
                 reads=[b_krb, rs_b], writes=[b_out[3]])
        for n_ in range(b * (TB // 128), (b + 1) * (TB // 128)):
            lo = n_ * 128 - b * TB
            psv, psvb, _ = psA.next()

            def mmv2(e, lo=lo, psv=psv):
                for k in range(2):
                    i = e.matmul(psv[:, 0:512], lhsT=ckv[:, k, lo:lo + 128], rhs=wkvu[:, k * 1024 + 512:k * 1024 + 1024], start=(k == 0), stop=(k == 1))
                return i
            S.op("tensor", mmv2, reads=[b_wkvu, b_ckv], writes=[psvb])
            S.op("scalar", lambda e, n_=n_, psv=psv: e.activation(out=v_s[:, n_, :], in_=psv[:, 0:512], func=AF.Copy), reads=[psvb], writes=[b_vs[n_]])
    for t_, d_, bb, nm in ((qa_s, L["qa_o"], [b_out[0]], "qa_o"), (qb_s, L["qb_o"], [b_out[1]], "qb_o"), (ka_s, L["ka_o"], [b_out[2]], "ka_o"),
                           (kb_s, L["kb_o"], [b_out[3]], "kb_o"), (v_s, L["v_o"], b_vs, "v_o")):
        cx.out_dmas.append(S.op("sync", lambda e, t_=t_, d_=d_: e.dma_start(out=d_, in_=t_[:]), reads=bb, dma_slot=nm))


def build_B(cx, L):
    S = cx.S
    (T, TB, NT, KC, ND) = (L[k] for k in ("T", "TB", "NT", "KC", "ND"))
    (psA, psN, psT, f32_ring, sq_ring, rstd_ring, hT, b_h, b_hh, ones, b_ones, epst, b_eps, kd, rv, b_kd, b_rv, wload, proj_fm, rope_apply, xT, b_x) = (
        L[k] for k in ("psA", "psN", "psT", "f32_ring", "sq_ring", "rstd_ring", "hT", "b_h", "b_hh", "ones", "b_ones", "epst", "b_eps", "kd", "rv", "b_kd", "b_rv",
                       "wload", "proj_fm", "rope_apply", "xT", "b_x"))
    cos_r, sin_r, b_cr, b_sr, rkTb, b_rkTb = (L[k] for k in ("cos_r", "sin_r", "b_cr", "b_sr", "rkTb", "b_rkTb"))
    E = HALO + T
    y = cx.sb([128, 16, T], BF16, "y")
    b_y = [Buf("y%d" % i) for i in range(16)]
    big1 = cx.sb([128, 4, E], F32, "big1")
    big2 = cx.sb([128, 4, T], F32, "big2")
    b_big1 = [Buf("big1_%d" % i) for i in range(4)]
    b_big2 = [Buf("big2_%d" % i) for i in range(4)]
    wsc = cx.sb([128, 4, 3], F32, "wsc"); wcf = cx.sb([128, 4, 31], F32, "wcf"); cfv = cx.sb([128, 4, 3], F32, "cfv"); gret = cx.sb([128, 4], F32, "gret")
    rdec = cx.sb([64, 4, ND + 1], F32, "rdec"); qdec = cx.sb([64, 4, 128], F32, "qdec"); dmask = cx.sb([128, 4, 128], F32, "dmask")
    b_sm = Buf("small")
    for t_, d_, nm in ((wsc, L["wsc_d"], "wsc"), (wcf, L["wcf_d"], "wcf"), (cfv, L["cfv_d"], "cfv"), (gret, L["gret_d"], "gret"),
                       (rdec, L["rdec_d"], "rdec"), (qdec, L["qdec_d"], "qdec"), (dmask, L["dmask_d"], "dmask")):
        S.op("sync", lambda e, t_=t_, d_=d_: e.dma_start(out=t_[:], in_=d_), writes=[b_sm], dma_slot=nm)
    S.op("sync", lambda e: e.dma_start(out=y[:, 0:4, :], in_=L["oat_d"]), writes=b_y[0:4], dma_slot="oat")

    def proj_ext(wt, wb, oc, evac_own, evac_halo, need_halo):
        ps_t, ps_b, _ = psA.next()
        mm, rd, n = proj_fm(wt, wb, 512, oc * 128, 128, 0, ps_t)
        S.op("tensor", mm, reads=rd, writes=[ps_b])
        evac_own(ps_t, ps_b)
        if need_halo:
            ps_t, ps_b, _ = psA.next()
            mm, rd, n = proj_fm(wt, wb, 512, oc * 128, 128, 0, ps_t, halo=True)
            S.op("tensor", mm, reads=rd, writes=[ps_b])
            evac_halo(ps_t, ps_b)

    wt, wb = wload(L["w_sc_d"][1], KC * 512)
    for oc in range(4):
        proj_ext(wt, wb, oc,
                 lambda ps_t, ps_b, oc=oc: S.op("scalar", lambda e: e.activation(out=big1[:, oc, HALO:E], in_=ps_t[:, 0:T], func=AF.Copy), reads=[ps_b], writes=[b_big1[oc]]),
                 lambda ps_t, ps_b, oc=oc: S.op("scalar", lambda e: e.activation(out=big1[:, oc, 0:HALO], in_=ps_t[:, 0:HALO], func=AF.Copy), reads=[ps_b], writes=[b_big1[oc]]), True)
    wt, wb = wload(L["w_sc_d"][2], KC * 512)
    for oc in range(4):
        proj_ext(wt, wb, oc,
                 lambda ps_t, ps_b, oc=oc: S.op("vector", lambda e: e.tensor_tensor(out=big1[:, oc, HALO:E], in0=ps_t[:, 0:T], in1=big1[:, oc, HALO:E], op=ALU.mult), reads=[ps_b, b_big1[oc]], writes=[b_big1[oc]]),
                 lambda ps_t, ps_b, oc=oc: S.op("vector", lambda e: e.tensor_tensor(out=big1[:, oc, 0:HALO], in0=ps_t[:, 0:HALO], in1=big1[:, oc, 0:HALO], op=ALU.mult), reads=[ps_b, b_big1[oc]], writes=[b_big1[oc]]), True)
    for oc in range(4):
        S.op("vector", lambda e, oc=oc: e.tensor_scalar(out=big2[:, oc, :], in0=big1[:, oc, HALO - 2:HALO - 2 + T], scalar1=wsc[:, oc, 0:1], scalar2=None, op0=ALU.mult),
             reads=[b_big1[oc], b_sm], writes=[b_big2[oc]])
        for k in (1, 2):
            S.op("vector", lambda e, oc=oc, k=k: e.scalar_tensor_tensor(out=big2[:, oc, :], in0=big1[:, oc, HALO - 2 + k:HALO - 2 + k + T], scalar=wsc[:, oc, k:k + 1], in1=big2[:, oc, :], op0=ALU.mult, op1=ALU.add),
                 reads=[b_big1[oc], b_sm, b_big2[oc]], writes=[b_big2[oc]])
    wt, wb = wload(L["w_sc_d"][0], KC * 512)
    for oc in range(4):
        proj_ext(wt, wb, oc,
                 lambda ps_t, ps_b, oc=oc: S.op("vector", lambda e: e.tensor_tensor(out=y[:, 4 + oc, :], in0=ps_t[:, 0:T], in1=big2[:, oc, :], op=ALU.mult), reads=[ps_b, b_big2[oc]], writes=[b_y[4 + oc]]),
                 None, False)

    if L['cfg'].get('stop') == 'sc':
        return _finish_B(cx, L, y, b_y)
    wt, wb = wload(L["w_cf_d"][0], KC * 512)
    for oc in range(4):
        proj_ext(wt, wb, oc,
                 lambda ps_t, ps_b, oc=oc: S.op("scalar", lambda e: e.activation(out=big1[:, oc, HALO:E], in_=ps_t[:, 0:T], func=AF.Copy), reads=[ps_b], writes=[b_big1[oc]]),
                 lambda ps_t, ps_b, oc=oc: S.op("scalar", lambda e: e.activation(out=big1[:, oc, 0:HALO], in_=ps_t[:, 0:HALO], func=AF.Copy), reads=[ps_b], writes=[b_big1[oc]]), True)
    wt, wb = wload(L["w_cf_d"][1], KC * 512)

    def glu(ps_t, ps_b, oc, lo, hi, n):
        sg, sgb, _ = f32_ring.next()
        S.op("scalar", lambda e: e.activation(out=sg[:, 0:n], in_=ps_t[:, 0:n], func=AF.Sigmoid), reads=[ps_b], writes=[sgb])
        S.op("vector", lambda e: e.tensor_tensor(out=big1[:, oc, lo:hi], in0=big1[:, oc, lo:hi], in1=sg[:, 0:n], op=ALU.mult), reads=[sgb, b_big1[oc]], writes=[b_big1[oc]])
    for oc in range(4):
        proj_ext(wt, wb, oc, lambda ps_t, ps_b, oc=oc: glu(ps_t, ps_b, oc, HALO, E, T), lambda ps_t, ps_b, oc=oc: glu(ps_t, ps_b, oc, 0, HALO, HALO), True)
    for oc in range(4):
        S.op("vector", lambda e, oc=oc: e.tensor_scalar(out=big2[:, oc, :], in0=big1[:, oc, 2:2 + T], scalar1=wcf[:, oc, 0:1], scalar2=cfv[:, oc, 0:1], op0=ALU.mult, op1=ALU.add),
             reads=[b_big1[oc], b_sm], writes=[b_big2[oc]])
        for k in range(1, 31):
            S.op("vector", lambda e, oc=oc, k=k: e.scalar_tensor_tensor(out=big2[:, oc, :], in0=big1[:, oc, 2 + k:2 + k + T], scalar=wcf[:, oc, k:k + 1], in1=big2[:, oc, :], op0=ALU.mult, op1=ALU.add),
                 reads=[b_big1[oc], b_sm, b_big2[oc]], writes=[b_big2[oc]])
    ubf = cx.sb([128, 4, T], BF16, "ubf")
    b_ubf = Buf("ubf")
    mu = cx.sb([128, T], F32, "mu")
    b_mu = Buf("mu")
    ps_t, ps_b, _ = psN.next()
    for oc in range(4):
        S.op("scalar", lambda e, oc=oc: e.activation(out=ubf[:, oc, :], in_=big2[:, oc, :], func=AF.Copy), reads=[b_big2[oc]], writes=[b_ubf])
        S.op("tensor", lambda e, oc=oc, ps_t=ps_t: e.matmul(ps_t[:, 0:T], lhsT=ones[:], rhs=ubf[:, oc, :], start=(oc == 0), stop=(oc == 3)), reads=[b_ubf, b_ones], writes=[ps_b])
    S.op("scalar", lambda e, ps_t=ps_t: e.activation(out=mu[:, :], in_=ps_t[:, 0:T], func=AF.Copy, scale=1.0 / 512), reads=[ps_b], writes=[b_mu])
    for oc in range(4):
        S.op("vector", lambda e, oc=oc: e.tensor_tensor(out=big2[:, oc, :], in0=big2[:, oc, :], in1=mu[:, :], op=ALU.subtract), reads=[b_big2[oc], b_mu], writes=[b_big2[oc]])
    ps_t, ps_b, _ = psN.next()
    for oc in range(4):
        sq_t, sq_b, _ = sq_ring.next()
        S.op("scalar", lambda e, oc=oc, sq_t=sq_t: e.activation(out=sq_t[:, 0:T], in_=big2[:, oc, :], func=AF.Square), reads=[b_big2[oc]], writes=[sq_b])
        S.op("tensor", lambda e, oc=oc, sq_t=sq_t, ps_t=ps_t: e.matmul(ps_t[:, 0:T], lhsT=ones[:], rhs=sq_t[:, 0:T], start=(oc == 0), stop=(oc == 3)), reads=[sq_b, b_ones], writes=[ps_b])
    S.op("scalar", lambda e, ps_t=ps_t: e.activation(out=mu[:, :], in_=ps_t[:, 0:T], func=AF.Sqrt, scale=1.0 / 512, bias=epst[:, 0:1]), reads=[ps_b, b_eps], writes=[b_mu])
    S.op("vector", lambda e: e.reciprocal(mu[:, :], mu[:, :]), reads=[b_mu], writes=[b_mu])
    for oc in range(4):
        S.op("vector", lambda e, oc=oc: e.tensor_tensor(out=big2[:, oc, :], in0=big2[:, oc, :], in1=mu[:, :], op=ALU.mult), reads=[b_big2[oc], b_mu], writes=[b_big2[oc]])
        S.op("scalar", lambda e, oc=oc: e.activation(out=y[:, 8 + oc, :], in_=big2[:, oc, :], func=AF.Silu, scale=cfv[:, oc, 1:2], bias=cfv[:, oc, 2:3]),
             reads=[b_big2[oc], b_sm], writes=[b_y[8 + oc]])

    if L['cfg'].get('stop') == 'cf':
        return _finish_B(cx, L, y, b_y)
    rqTb = cx.sb([64, 4, T], BF16, "rqTb"); qdTb = cx.sb([64, 4, T], BF16, "qdTb")
    rgs = cx.sb([128, 4, T], F32, "rgs")
    b_rq = [Buf("rq%d" % c) for c in range(4)]; b_qd = [Buf("qd%d" % c) for c in range(4)]; b_rg = [Buf("rg%d" % c) for c in range(4)]
    wt, wb = wload(L["w_rq_d"], KC * 256)
    for h in range(4):
        ps_t, ps_b, _ = psA.next()
        mm, rd, n = proj_fm(wt, wb, 256, h * 64, 64, 0, ps_t)
        S.op("tensor", mm, reads=rd, writes=[ps_b])
        raw, rawb, _ = f32_ring.next()
        S.op("scalar", lambda e, raw=raw, ps_t=ps_t: e.activation(out=raw[0:64, 0:T], in_=ps_t[0:64, 0:T], func=AF.Copy), reads=[ps_b], writes=[rawb])
        rq32, rq32b, _ = f32_ring.next()
        rope_apply(lambda raw=raw: raw[0:64, 0:T], 64, T, lambda: cos_r[0:64, 0:T], lambda: sin_r[0:64, 0:T], [b_cr, b_sr], lambda rq32=rq32: rq32[0:64, 0:T], [rawb], [rq32b])
        S.op("scalar", lambda e, h=h, rq32=rq32: e.activation(out=rqTb[:, h, :], in_=rq32[0:64, 0:T], func=AF.Copy), reads=[rq32b], writes=[b_rq[h]])
        for n_ in range(NT):
            S.op("vector", lambda e, h=h, n_=n_, rq32=rq32: e.tensor_tensor(out=qdTb[:, h, n_ * 128:(n_ + 1) * 128], in0=rq32[0:64, n_ * 128:(n_ + 1) * 128], in1=qdec[:, h, :], op=ALU.mult),
                 reads=[rq32b, b_sm], writes=[b_qd[h]])
    wt, wb = wload(L["w_rg_d"], KC * 512)
    for oc in range(4):
        proj_ext(wt, wb, oc, lambda ps_t, ps_b, oc=oc: S.op("scalar", lambda e: e.activation(out=rgs[:, oc, :], in_=ps_t[:, 0:T], func=AF.Silu), reads=[ps_b], writes=[b_rg[oc]]), None, False)
    stt = cx.sb([64, 4, 128], F32, "stt"); stb = cx.sb([64, 4, 128], BF16, "stb")
    b_st, b_stb = Buf("st"), Buf("stb")
    sp_ring = Ring(cx, 2, [64, 4, 128], F32, "sp")
    S.op("vector", lambda e: e.memset(stt[:], 0.0), writes=[b_st])
    for d in range(ND):
        sp_t, sp_b, sp_n = sp_ring.next()
        S.op("sync", lambda e, d=d, sp_t=sp_t: e.dma_start(out=sp_t[:], in_=L["sprev_d"][:, d]), writes=[sp_b], dma_slot=sp_n)
        for h in range(4):
            S.op("vector", lambda e, d=d, h=h, sp_t=sp_t: e.scalar_tensor_tensor(out=stt[:, h, :], in0=sp_t[:, h, :], scalar=rdec[:, h, d:d + 1], in1=stt[:, h, :], op0=ALU.mult, op1=ALU.add),
                 reads=[sp_b, b_sm, b_st], writes=[b_st])
    S.op("scalar", lambda e: e.activation(out=stb[:], in_=stt[:], func=AF.Copy), reads=[b_st], writes=[b_stb])
    scm_ring = Ring(cx, 2, [128, 4, 128], BF16, "scm")
    for n_ in range(NT):
        tsl = slice(n_ * 128, (n_ + 1) * 128)
        ps_s, ps_sb, _ = psA.next()

        def mmsc(e, ps_s=ps_s, tsl=tsl):
            for h in range(4):
                i = e.matmul(ps_s[:, h * 128:(h + 1) * 128], lhsT=rkTb[:, h, tsl], rhs=rqTb[:, h, tsl], start=True, stop=True)
            return i
        S.op("tensor", mmsc, reads=b_rkTb + b_rq, writes=[ps_sb])
        scm, scmb, _ = scm_ring.next()
        S.op("vector", lambda e, ps_s=ps_s, scm=scm: e.tensor_tensor(out=scm[:], in0=ps_s[:, 0:512], in1=dmask[:], op=ALU.mult), reads=[ps_sb, b_sm], writes=[scmb])
        ps_o, ps_ob, _ = psA.next()

        def mmo(e, ps_o=ps_o, scm=scm, n_=n_, tsl=tsl):
            for h in range(4):
                e.matmul(ps_o[:, h * 128:(h + 1) * 128], lhsT=rv[:, n_, h * 128:(h + 1) * 128], rhs=scm[:, h, :], start=True, stop=False)
                i = e.matmul(ps_o[:, h * 128:(h + 1) * 128], lhsT=stb[:, h, :], rhs=qdTb[:, h, tsl], start=False, stop=True)
            return i
        S.op("tensor", mmo, reads=[b_rv[n_], scmb, b_stb] + b_qd, writes=[ps_ob])
        ps_u, ps_ub, _ = psA.next()

        def mmu(e, ps_u=ps_u, n_=n_):
            for h in range(4):
                i = e.matmul(ps_u[0:64, h * 128:(h + 1) * 128], lhsT=kd[:, n_, h * 64:(h + 1) * 64], rhs=rv[:, n_, h * 128:(h + 1) * 128], start=True, stop=True)
            return i
        S.op("tensor", mmu, reads=[b_kd[n_], b_rv[n_]], writes=[ps_ub])
        for h in range(4):
            S.op("vector", lambda e, h=h, ps_u=ps_u: e.scalar_tensor_tensor(out=stt[:, h, :], in0=stt[:, h, :], scalar=rdec[:, h, ND:ND + 1], in1=ps_u[0:64, h * 128:(h + 1) * 128], op0=ALU.mult, op1=ALU.add),
                 reads=[b_st, b_sm, ps_ub], writes=[b_st])
        S.op("scalar", lambda e: e.activation(out=stb[:], in_=stt[:], func=AF.Copy), reads=[b_st], writes=[b_stb])
        sq_t, sq_b, _ = sq_ring.next()
        S.op("scalar", lambda e, sq_t=sq_t, ps_o=ps_o: e.activation(out=sq_t[:, 0:512], in_=ps_o[:, 0:512], func=AF.Square), reads=[ps_ob], writes=[sq_b])
        ps_n, ps_nb, _ = psN.next()
        S.op("tensor", lambda e, sq_t=sq_t, ps_n=ps_n: e.matmul(ps_n[:, 0:512], lhsT=ones[:], rhs=sq_t[:, 0:512], start=True, stop=True), reads=[sq_b, b_ones], writes=[ps_nb])
        rs_t, rs_b, _ = rstd_ring.next()
        S.op("scalar", lambda e, rs_t=rs_t, ps_n=ps_n: e.activation(out=rs_t[:, 0:512], in_=ps_n[:, 0:512], func=AF.Sqrt, scale=1.0 / 128, bias=epst[:, 0:1]), reads=[ps_nb, b_eps], writes=[rs_b])
        S.op("vector", lambda e, rs_t=rs_t: e.reciprocal(rs_t[:, 0:512], rs_t[:, 0:512]), reads=[rs_b], writes=[rs_b])
        S.op("vector", lambda e, rs_t=rs_t, ps_o=ps_o: e.tensor_tensor(out=rs_t[:, 0:512], in0=ps_o[:, 0:512], in1=rs_t[:, 0:512], op=ALU.mult), reads=[ps_ob, rs_b], writes=[rs_b])
        for h in range(4):
            S.op("vector", lambda e, h=h, rs_t=rs_t, tsl=tsl: e.scalar_tensor_tensor(out=y[:, 12 + h, tsl], in0=rs_t[:, h * 128:(h + 1) * 128], scalar=gret[:, h:h + 1], in1=rgs[:, h, tsl], op0=ALU.mult, op1=ALU.mult),
                 reads=[rs_b, b_sm, b_rg[h]], writes=[b_y[12 + h]])

    return _finish_B(cx, L, y, b_y)


def _finish_B(cx, L, y, b_y):
    S = cx.S
    (T, KC, psA, wload, xT, b_x) = (L[k] for k in ('T', 'KC', 'psA', 'wload', 'xT', 'b_x'))
    for og in range(KC // 2):
        wt, wb = wload(L["w_o_d"][og], 16 * 256)
        for oc in range(2):
            ko = og * 2 + oc
            ps_t, ps_b, _ = psA.next()

            def mmo2(e, oc=oc, ps_t=ps_t, wt=wt):
                for k in range(16):
                    i = e.matmul(ps_t[:, 0:T], lhsT=wt[:, k * 256 + oc * 128:k * 256 + (oc + 1) * 128], rhs=y[:, k, :], start=(k == 0), stop=(k == 15))
                return i
            S.op("tensor", mmo2, reads=[wb] + b_y, writes=[ps_b])
            S.op("vector", lambda e, ko=ko, ps_t=ps_t: e.tensor_tensor(out=xT[:, ko, :], in0=ps_t[:, 0:T], in1=xT[:, ko, :], op=ALU.add), reads=[ps_b, b_x[ko][0]], writes=[b_x[ko][0]])
            cx.out_dmas.append(S.op("sync", lambda e, ko=ko: e.dma_start(out=L["xo_d"][:, ko, :], in_=xT[:, ko, :]), reads=[b_x[ko][0]], dma_slot="xo%d" % ko))


def mix_consts(cfg):
    T, ND = cfg["T"], cfg["NCORE"] - 1
    NT = T // 128
    inv_mla = (10000.0 ** (-np.arange(0, 64, 2, dtype=np.float32) / 64)).astype(np.float32)
    inv_ret = (1.0 / (10000.0 ** np.linspace(0.0, 1.0, 32, dtype=np.float32))).astype(np.float32)
    cst = np.zeros((128, 8), np.float32)
    cst[:, 0] = np.tile(inv_mla, 4)
    cst[:, 1] = np.tile(inv_ret, 4)
    pm = np.zeros((128, 128), np.float32)
    for blk in range(2):
        for i in range(32):
            pm[blk * 64 + i + 32, blk * 64 + i] = -1.0
            pm[blk * 64 + i, blk * 64 + i + 32] = 1.0
    gam = (1.0 - 2.0 ** (-5.0 - np.arange(4, dtype=np.float64)))
    s = np.arange(128, dtype=np.float64)
    kdecA = np.stack([np.stack([gam[h] ** (T - 1 - (n * 128 + s)) for h in range(4)], -1) for n in range(NT)], 1)
    kdecB = np.stack([gam[h] ** (127 - s) for h in range(4)], -1)[:, None, :]
    rdec = np.zeros((64, 4, ND + 1))
    qdec = np.zeros((64, 4, 128))
    for h in range(4):
        rdec[:, h, :ND] = gam[h] ** (T * np.arange(ND, dtype=np.float64))[None, :]
        rdec[:, h, ND] = gam[h] ** 128
        qdec[:, h, :] = (gam[h] ** (s + 1))[None, :]
    dmask = np.zeros((128, 4, 128))
    for h in range(4):
        rel = s[None, :] - s[:, None]
        dmask[:, h, :] = np.where(rel >= 0, gam[h] ** np.maximum(rel, 0), 0.0)
    f = lambda a: np.ascontiguousarray(a.astype(np.float32))
    return dict(cst=cst, pmat=pm, kdecA=f(kdecA), kdecB=f(kdecB), rdec=f(rdec), qdec=f(qdec), dmask=f(dmask),
                ident=np.eye(128, dtype=np.float32))


def mix_weights(cfg, P, mode):
    c = mix_consts(cfg)
    w_in = P["w_in"]
    cols = lambda name, n: np.arange(OFF[name], OFF[name] + n)
    out = dict(g_mix=vec_fm(P["g_mix"]), cst=c["cst"], pmat=c["pmat"], ident=c["ident"],
               w_rk=wtile(w_in, cols("rk", 256)), w_rv=wtile(w_in, cols("rv", 512)))
    if mode == "A":
        kv_cols = np.concatenate([np.concatenate([np.arange(h * 256, h * 256 + 128) for h in range(4)]),
                                  np.concatenate([np.arange(h * 256 + 128, h * 256 + 256) for h in range(4)])])
        pad = lambda v: np.ascontiguousarray(np.stack([v[:128], np.concatenate([v[128:], np.zeros(64, np.float32)])], 1))
        out.update(kdec=c["kdecA"], w_zq=wtile(w_in, cols("zq", 512)), w_zkv=wtile(w_in, np.arange(512, 832)),
                   g_qa=vec_fm(P["g_qa"]), g_kva=vec_fm(P["g_kva"]), w_q_up=wtile(P["w_q_up"], np.arange(768)),
                   w_kv_up=wtile(P["w_kv_up"], kv_cols), g_qn=pad(P["g_qn"]), g_kn=pad(P["g_kn"]))
    else:
        out.update(kdec=c["kdecB"], rdec=c["rdec"], qdec=c["qdec"], dmask=c["dmask"],
                   w_scp=np.stack([wtile(w_in, cols(n, 512)) for n in ("sc_b", "sc_c", "sc_h")]),
                   w_cfp=np.stack([wtile(w_in, cols(n, 512)) for n in ("cf_a", "cf_g")]),
                   w_rq=wtile(w_in, cols("rq", 256)), w_rg=wtile(w_in, cols("rg", 512)),
                   w_sc=np.ascontiguousarray(P["w_sc"].T.reshape(4, 128, 3).transpose(1, 0, 2)),
                   w_cf=np.ascontiguousarray(P["w_cf"].T.reshape(4, 128, 31).transpose(1, 0, 2)),
                   cf_vec=np.ascontiguousarray(np.stack([vec_fm(P["b_cf"]), vec_fm(P["g_cf_ln"]), vec_fm(P["b_cf_ln"])], -1)),
                   g_ret=np.ascontiguousarray(P["g_ret"].T),
                   w_o=np.stack([wtile(P["w_o"], np.arange(og * 256, (og + 1) * 256)) for og in range(P["w_o"].shape[1] // 256)]))
    return out


_PROGS = {}


def _prog(name, builder):
    if name not in _PROGS:
        _PROGS[name] = builder()
    return _PROGS[name]


def _run(nc, in_maps):
    res = run_bass_kernel_spmd(nc, in_maps, core_ids=list(range(len(in_maps))))
    return res.results


def kernel(x, p, positions, g_mix, w_in, g_qa, g_kva, w_q_up, w_kv_up, g_qn, g_kn,
           w_sc, w_cf, b_cf, g_cf_ln, b_cf_ln, g_ret, w_o, g_ffn, w_up, w_ffn_conv,
           w_down, g_pe, w_pe, w_pg):
    f32 = lambda a: np.ascontiguousarray(np.asarray(a, dtype=np.float32))
    x = f32(x)[0]
    p = f32(p)
    pos = np.ascontiguousarray(np.asarray(positions, dtype=np.int32))
    S_, D = x.shape
    NSH, TS = 16, 512
    mcfg = mix_cfg(D=D, T=TS, TB=TS, NCORE=NSH, NLOC=2)
    fcfg = ffn_cfg()
    acfg = attn_cfg(S=S_, QB=TS)
    ncA = _prog("A", lambda: build_mix(mcfg, "A"))
    ncB = _prog("B", lambda: build_mix(mcfg, "B"))
    ncT = _prog("T", lambda: build_attn(acfg))
    ncF = _prog("F", lambda: build_ffn(fcfg))
    masks = [attn_masks(acfg, par).astype(ml_dtypes.bfloat16) for par in (0, 1)]
    xs = [fm(x[s * TS:(s + 1) * TS]) for s in range(NSH)]
    L = len(g_mix)
    for i in range(L):
        P = dict(g_mix=f32(g_mix[i]), w_in=f32(w_in[i]), g_qa=f32(g_qa[i]), g_kva=f32(g_kva[i]), w_q_up=f32(w_q_up[i]),
                 w_kv_up=f32(w_kv_up[i]), g_qn=f32(g_qn[i]), g_kn=f32(g_kn[i]), w_sc=f32(w_sc[i]), w_cf=f32(w_cf[i]),
                 b_cf=f32(b_cf[i]), g_cf_ln=f32(g_cf_ln[i]), b_cf_ln=f32(b_cf_ln[i]), g_ret=f32(g_ret[i]), w_o=f32(w_o[i]))
        wA = mix_weights(mcfg, P, "A")
        wB = mix_weights(mcfg, P, "B")
        maps = []
        for c in range(8):
            m = dict(wA)
            for l in range(2):
                s = 2 * c + l
                m["xT_%d" % l] = xs[s]
                m["pos_%d" % l] = np.ascontiguousarray(pos[:, s * TS:(s + 1) * TS])
            maps.append(m)
        resA = _run(ncA, maps)
        outA = []
        for c in range(8):
            for l in range(2):
                outA.append({k: resA[c]["%s_%d" % (k, l)] for k in ("qa_o", "qb_o", "ka_o", "kb_o", "v_o", "s_o")})
        maps = []
        for c in range(8):
            h, par = c % 4, c // 4
            blocks = [2 * j + par for j in range(NSH // 2)]
            m = dict(qa=np.ascontiguousarray(np.stack([outA[b_]["qa_o"][:, h, :] for b_ in blocks], 1)),
                     qb=np.ascontiguousarray(np.stack([outA[b_]["qb_o"][:, h, :] for b_ in blocks], 1)),
                     ka=np.ascontiguousarray(np.concatenate([outA[s]["ka_o"][:, h, :] for s in range(NSH)], 1)),
                     kb=np.ascontiguousarray(np.concatenate([outA[s]["kb_o"][:, h, :] for s in range(NSH)], 1)),
                     v=np.ascontiguousarray(np.concatenate([outA[s]["v_o"][:, :, h * 128:(h + 1) * 128] for s in range(NSH)], 1)),
                     mask=masks[par])
            maps.append(m)
        outT = _run(ncT, maps)
        maps = []
        for c in range(8):
            m = dict(wB)
            for l in range(2):
                s = 2 * c + l
                par, j = s % 2, s // 2
                oat = np.ascontiguousarray(np.stack([outT[h + 4 * par]["o"][:, j, :] for h in range(4)], 1))
                xh = np.zeros((128, D // 128, HALO), np.float32) if s == 0 else np.ascontiguousarray(xs[s - 1][:, :, TS - HALO:])
                sprev = np.zeros((64, NSH - 1, 4, 128), np.float32)
                for d in range(NSH - 1):
                    if s - 1 - d >= 0:
                        sprev[:, d] = outA[s - 1 - d]["s_o"]
                for k, v in dict(xT=xs[s], pos=np.ascontiguousarray(pos[:, s * TS:(s + 1) * TS]), xh=xh, s_prev=sprev, o_attn=oat).items():
                    m["%s_%d" % (k, l)] = v
            maps.append(m)
        resB = _run(ncB, maps)
        xm = [resB[c]["xo_%d" % l] for c in range(8) for l in range(2)]
        wF = ffn_host_inputs(fcfg, f32(w_up[i]), f32(w_ffn_conv[i]), f32(w_down[i]), f32(w_pe[i]), f32(w_pg[i]), f32(g_ffn[i]), f32(g_pe[i]))
        maps = []
        for c in range(8):
            xT = np.ascontiguousarray(np.concatenate([xm[2 * c], xm[2 * c + 1]], 2))
            xh = np.zeros((128, D // 128, 2), np.float32) if c == 0 else np.ascontiguousarray(xm[2 * c - 1][:, :, TS - 2:])
            m = dict(wF); m.update(xT=xT, xh=xh, pT=fm(p[i, 0, c * 1024:(c + 1) * 1024]))
            maps.append(m)
        outF = _run(ncF, maps)
        xs = []
        for c in range(8):
            xo = outF[c]["xo"]
            xs.append(np.ascontiguousarray(xo[:, :, :TS])); xs.append(np.ascontiguousarray(xo[:, :, TS:]))
    out = np.concatenate([unfm(a) for a in xs], 0)
    return out[None].astype(np.float32)
```
